# Optimizing a Trainium2 kernel written in Bass

```python
import math
import jax, jax.numpy as jnp
from jax import lax
import numpy as np

D_MODEL = 1024
BATCH = 16
SEQ = 2048
DEPTH = 2

SSD_HEADS = 16
SSD_HEAD_DIM = 64
SSD_WIDTH = SSD_HEADS * SSD_HEAD_DIM
SSD_GROUPS = 4
SSD_STATE = 128
CONV_K = 4
SSD_CHUNK = 128
CONV_DIM = SSD_WIDTH + 2 * SSD_GROUPS * SSD_STATE

ATT_HEADS = 16
ATT_HEAD_DIM = 64
ATT_WIDTH = ATT_HEADS * ATT_HEAD_DIM
ATT_BRANCHES = ((128, 1), (512, 4), (2048, 16))
BAND_BLOCK = 128

MIX_WIDTH = SSD_WIDTH + ATT_WIDTH
IN_PROJ = SSD_WIDTH + CONV_DIM + SSD_HEADS + 3 * ATT_WIDTH
D_FF = 2816
EPS = 1e-6

kernel_name = "hymba_ssd_dilated_macaron"


def rms_norm(x, w):
    xf = x.astype(jnp.float32)
    y = xf * lax.rsqrt(jnp.mean(xf * xf, axis=-1, keepdims=True) + EPS)
    return (y * w.astype(jnp.float32)).astype(x.dtype)


def swiglu(x, w_gate, w_up, w_down):
    return (jax.nn.silu(x @ w_gate) * (x @ w_up)) @ w_down


def causal_depthwise_conv(x, w, b):
    c = x.shape[-1]
    y = lax.conv_general_dilated(
        x, w[:, None, :], window_strides=(1,), padding=[(CONV_K - 1, 0)],
        dimension_numbers=("NWC", "WIO", "NWC"), feature_group_count=c)
    return y + b


def ssd_mixer(xbc, dt_raw, z, dt_bias, a_log, d_skip, norm_w):
    out_dtype = z.dtype
    b, s, _ = xbc.shape
    H, P, G, N, L = SSD_HEADS, SSD_HEAD_DIM, SSD_GROUPS, SSD_STATE, SSD_CHUNK
    R = H // G
    nc = s // L
    xbc = xbc.astype(jnp.float32)
    xs, bm, cm = jnp.split(xbc, [SSD_WIDTH, SSD_WIDTH + G * N], axis=-1)
    dt = jax.nn.softplus(dt_raw.astype(jnp.float32) + dt_bias.astype(jnp.float32))
    a = -jnp.exp(a_log.astype(jnp.float32))
    xh = xs.reshape(b, s, H, P)
    X = (xh * dt[..., None]).reshape(b, nc, L, G, R, P)
    adt = (dt * a).reshape(b, nc, L, G, R).transpose(0, 3, 4, 1, 2)
    a_cum = jnp.cumsum(adt, axis=-1)
    Bc = bm.reshape(b, nc, L, G, N)
    Cc = cm.reshape(b, nc, L, G, N)
    causal = jnp.tril(jnp.ones((L, L), dtype=bool))
    seg = a_cum[..., :, None] - a_cum[..., None, :]
    lmat = jnp.where(causal, jnp.exp(jnp.where(causal, seg, 0.0)), 0.0)
    cb = jnp.einsum("bclgn,bcsgn->bgcls", Cc, Bc)
    y_diag = jnp.einsum("bgcls,bgrcls,bcsgrp->bclgrp", cb, lmat, X)
    decay_states = jnp.exp(a_cum[..., -1:] - a_cum)
    states = jnp.einsum("bclgn,bgrcl,bclgrp->bcgrpn", Bc, decay_states, X)
    chunk_decay = jnp.exp(a_cum[..., -1])

    def step(h, inp):
        st, dec = inp
        return h * dec[..., None, None] + st, h

    h0 = jnp.zeros((b, G, R, P, N), jnp.float32)
    _, prev = lax.scan(step, h0, (states.transpose(1, 0, 2, 3, 4, 5),
                                  chunk_decay.transpose(3, 0, 1, 2)))
    y_off = jnp.einsum("bclgn,cbgrpn,bgrcl->bclgrp", Cc, prev, jnp.exp(a_cum))
    y = (y_diag + y_off).reshape(b, s, H, P) + xh * d_skip.astype(jnp.float32)[:, None]
    y = y.reshape(b, s, SSD_WIDTH) * jax.nn.silu(z.astype(jnp.float32))
    yg = y.reshape(b, s, G, SSD_WIDTH // G)
    yg = yg * lax.rsqrt(jnp.mean(yg * yg, axis=-1, keepdims=True) + EPS)
    y = yg.reshape(b, s, SSD_WIDTH) * norm_w.astype(jnp.float32)
    return y.astype(out_dtype)


def dilated_branch(q, k, v, dilation, back):
    b, s, h, e = q.shape
    n = s // dilation
    nb = -(-n // BAND_BLOCK)
    npad = nb * BAND_BLOCK
    blk = BAND_BLOCK

    def blocks(t):
        t = t.reshape(b, n, dilation, h, e)
        t = jnp.pad(t, ((0, 0), (0, npad - n), (0, 0), (0, 0), (0, 0)))
        return t.reshape(b, nb, blk, dilation, h, e)

    def with_prev(t):
        prev = jnp.pad(t[:, :-1], ((0, 0), (1, 0), (0, 0), (0, 0), (0, 0), (0, 0)))
        return jnp.concatenate([prev, t], axis=2)

    qb = blocks(q)
    kk = with_prev(blocks(k))
    vv = with_prev(blocks(v))
    scores = jnp.einsum("bnqrhe,bnkrhe->bnrhqk", qb, kk).astype(jnp.float32)
    scores = scores * (1.0 / math.sqrt(e))
    qi = jnp.arange(blk)[:, None]
    kj = jnp.arange(2 * blk)[None, :]
    dist = qi + blk - kj
    bidx = jnp.arange(nb)[:, None, None]
    valid = (dist >= 0) & (dist <= back) & (bidx * blk - blk + kj >= 0)
    scores = jnp.where(valid[None, :, None, None], scores, -jnp.inf)
    m = jnp.max(scores, axis=-1, keepdims=True)
    p = jnp.exp(scores - m)
    den = jnp.sum(p, axis=-1, keepdims=True)
    o = jnp.einsum("bnrhqk,bnkrhe->bnqrhe", (p / den).astype(v.dtype), vv)
    lse = (m + jnp.log(den))[..., 0]
    o = o.reshape(b, npad, dilation, h, e)[:, :n].reshape(b, s, h, e)
    lse = lse.transpose(0, 1, 4, 2, 3).reshape(b, npad, dilation, h)[:, :n].reshape(b, s, h)
    return o, lse


def dilated_attention(q, k, v):
    outs, lses = [], []
    for window, dilation in ATT_BRANCHES:
        o, lse = dilated_branch(q, k, v, dilation, window // dilation)
        outs.append(o)
        lses.append(lse)
    wts = jax.nn.softmax(jnp.stack(lses, axis=0), axis=0)
    y = jnp.einsum("kbsh,kbshe->bshe", wts.astype(q.dtype), jnp.stack(outs, axis=0))
    return y


def setup_inputs(seed: int = 0) -> dict:
    key = jax.random.key(seed)
    ks = jax.random.split(key, 24)
    f = jnp.float32

    def normal(k, shape, scale):
        return jax.random.normal(k, shape, f) * scale

    def gain(k, shape):
        return 1.0 + 0.02 * jax.random.normal(k, shape, f)

    dt = jnp.exp(jax.random.uniform(ks[10], (DEPTH, SSD_HEADS), f)
                 * (math.log(0.1) - math.log(0.001)) + math.log(0.001))
    return {
        "x": jax.random.normal(ks[0], (BATCH, SEQ, D_MODEL), f),
        "ffn1_norm": gain(ks[1], (DEPTH, D_MODEL)),
        "ffn1_w_gate": normal(ks[2], (DEPTH, D_MODEL, D_FF), D_MODEL ** -0.5),
        "ffn1_w_up": normal(ks[3], (DEPTH, D_MODEL, D_FF), D_MODEL ** -0.5),
        "ffn1_w_down": normal(ks[4], (DEPTH, D_FF, D_MODEL), D_FF ** -0.5),
        "mix_norm": gain(ks[5], (DEPTH, D_MODEL)),
        "w_in": normal(ks[6], (DEPTH, D_MODEL, IN_PROJ), D_MODEL ** -0.5),
        "conv_w": normal(ks[7], (DEPTH, CONV_K, CONV_DIM), CONV_K ** -0.5),
        "conv_b": normal(ks[8], (DEPTH, CONV_DIM), 0.02),
        "dt_bias": dt + jnp.log(-jnp.expm1(-dt)),
        "a_log": jnp.log(jax.random.uniform(ks[11], (DEPTH, SSD_HEADS), f, 1.0, 16.0)),
        "d_skip": gain(ks[12], (DEPTH, SSD_HEADS)),
        "ssd_norm": gain(ks[13], (DEPTH, SSD_WIDTH)),
        "q_norm": gain(ks[14], (DEPTH, ATT_HEAD_DIM)),
        "k_norm": gain(ks[15], (DEPTH, ATT_HEAD_DIM)),
        "w_out": normal(ks[16], (DEPTH, MIX_WIDTH, D_MODEL), MIX_WIDTH ** -0.5),
        "ffn2_norm": gain(ks[17], (DEPTH, D_MODEL)),
        "ffn2_w_gate": normal(ks[18], (DEPTH, D_MODEL, D_FF), D_MODEL ** -0.5),
        "ffn2_w_up": normal(ks[19], (DEPTH, D_MODEL, D_FF), D_MODEL ** -0.5),
        "ffn2_w_down": normal(ks[20], (DEPTH, D_FF, D_MODEL), D_FF ** -0.5),
    }


def reference(x, ffn1_norm, ffn1_w_gate, ffn1_w_up, ffn1_w_down, mix_norm, w_in,
              conv_w, conv_b, dt_bias, a_log, d_skip, ssd_norm, q_norm, k_norm,
              w_out, ffn2_norm, ffn2_w_gate, ffn2_w_up, ffn2_w_down):
    b, s, _ = x.shape
    splits = np.cumsum([SSD_WIDTH, CONV_DIM, SSD_HEADS, ATT_WIDTH, ATT_WIDTH]).tolist()
    for i in range(DEPTH):
        x = x + 0.5 * swiglu(rms_norm(x, ffn1_norm[i]), ffn1_w_gate[i], ffn1_w_up[i], ffn1_w_down[i])
        h = rms_norm(x, mix_norm[i])
        proj = h @ w_in[i]
        z, xbc, dt_raw, q, k, v = jnp.split(proj, splits, axis=-1)
        xbc = jax.nn.silu(causal_depthwise_conv(xbc, conv_w[i], conv_b[i]))
        y_ssd = ssd_mixer(xbc, dt_raw, z, dt_bias[i], a_log[i], d_skip[i], ssd_norm[i])
        q = rms_norm(q.reshape(b, s, ATT_HEADS, ATT_HEAD_DIM), q_norm[i])
        k = rms_norm(k.reshape(b, s, ATT_HEADS, ATT_HEAD_DIM), k_norm[i])
        v = v.reshape(b, s, ATT_HEADS, ATT_HEAD_DIM)
        y_att = dilated_attention(q, k, v).reshape(b, s, ATT_WIDTH)
        x = x + jnp.concatenate([y_ssd, y_att], axis=-1) @ w_out[i]
        x = x + 0.5 * swiglu(rms_norm(x, ffn2_norm[i]), ffn2_w_gate[i], ffn2_w_up[i], ffn2_w_down[i])
    return x
```

```python
import numpy as np
from contextlib import ExitStack
import concourse.bass as bass
import concourse.mybir as mybir
from concourse.bass_utils import run_bass_kernel_spmd

F32 = mybir.dt.float32
BF16 = mybir.dt.bfloat16
AF = mybir.ActivationFunctionType
ALU = mybir.AluOpType

S = 2048
D = 1024
DFF = 2816
NCH = 8
EPS = 1e-6
N_CORES = 8
SEQ_PER_CORE = 2

Z0, X0, B0, C0, DT0, Q0, K0, V0 = 0, 1024, 2048, 2560, 3072, 3088, 4112, 5136

PV_FFN1, PV_MIX, PV_FFN2, PV_SSDN, PV_DSKIP, PV_QN, PV_KN, PV_CB, PV_CW, PV_DTB, PV_ALOG, NV = (
    0, 8, 16, 24, 32, 40, 41, 42, 58, 122, 138, 154)

ARENA_BYTES = 97 * 1024
FFN_GROUPS = [(0, 6), (6, 12), (12, 17), (17, 22)]
N_DMA_SEM = {"pool": 24, "sp": 20}
FORCE_FUSE = False
EMBED_WAIT = True


def _esz(dt):
    return 4 if dt == F32 else 2


class _Op:
    __slots__ = ("eng", "fn", "waits", "sig", "dma", "vc")

    def __init__(self, eng, fn):
        self.eng = eng
        self.fn = fn
        self.waits = []
        self.sig = False
        self.dma = None


class Prog:
    ENGS = ("pe", "act", "dve", "pool", "sp")

    def __init__(self, nc):
        self.nc = nc
        self.ops = {e: [] for e in self.ENGS}
        self.recs = {}
        self.tinfo = {}
        self.waited = {e: {} for e in self.ENGS}
        self.dma_rr = {"pool": 0, "sp": 0}
        self.dma_cnt = {}
        self.out_tokens = []
        self.evc = {e: {} for e in self.ENGS}
        self.dma_vc = {}

    def register(self, name, pstride_bytes):
        self.tinfo[name] = pstride_bytes
        self.recs[name] = []

    def region(self, ap):
        name = ap.tensor.name
        pb = self.tinfo.get(name)
        if pb is None:
            return None
        esz = _esz(ap.dtype)
        pstride = pb // esz
        off = ap.offset
        dims = ap.ap
        p0 = off // pstride
        fo = off - p0 * pstride
        ext = 0
        for st, cn in dims[1:]:
            ext += (cn - 1) * abs(st)
        if name.startswith("ps"):
            return (name, 0, 128, 0, 2048)
        return (name, p0, p0 + dims[0][1], fo * esz, (fo + ext + 1) * esz)

    def add(self, eng, fn, reads, writes, dma=False, is_out=False):
        op = _Op(eng, fn)
        deps = {}
        rregs = [r for r in (self.region(a) for a in reads) if r is not None]
        wregs = [r for r in (self.region(a) for a in writes) if r is not None]
        for (name, p0, p1, b0, b1) in rregs:
            isps = name.startswith("ps")
            for r in self.recs[name]:
                if (r[2] or (isps and r[0] != eng)) and r[3] < p1 and p0 < r[4] and r[5] < b1 and b0 < r[6]:
                    if r[0] == "pe" and eng == "pe":
                        continue
                    if deps.get(r[0], -1) < r[1]:
                        deps[r[0]] = r[1]
        for (name, p0, p1, b0, b1) in wregs:
            for r in self.recs[name]:
                if r[3] < p1 and p0 < r[4] and r[5] < b1 and b0 < r[6]:
                    if r[0] == eng and eng == "pe":
                        continue
                    if deps.get(r[0], -1) < r[1]:
                        deps[r[0]] = r[1]
        if dma:
            k = self.dma_rr[eng]
            self.dma_rr[eng] = (k + 1) % N_DMA_SEM[eng]
            key = ("dma", eng, k)
            m = self.dma_cnt.get(key, 0) + 1
            self.dma_cnt[key] = m
            if m > 1:
                if deps.get(key, -1) < 16 * (m - 1):
                    deps[key] = 16 * (m - 1)
            op.dma = (key, 16 * m)
            mykey, myval = key, 16 * m
            if is_out:
                self.out_tokens.append((key, 16 * m))
        else:
            mykey, myval = eng, len(self.ops[eng])
        w = self.evc[eng]
        for key, val in sorted(deps.items(), key=lambda kv: str(kv[0])):
            if w.get(key, -1) >= val:
                continue
            op.waits.append((key, val))
            if isinstance(key, tuple):
                src = self.dma_vc.get((key, val), {})
            else:
                self.ops[key][val].sig = True
                src = self.ops[key][val].vc
            for k2, v2 in src.items():
                if w.get(k2, -1) < v2:
                    w[k2] = v2
            if w.get(key, -1) < val:
                w[key] = val
        if dma:
            self.dma_vc[(mykey, myval)] = dict(w)
            op.vc = None
        else:
            op.vc = dict(w)
            op.vc[eng] = max(op.vc.get(eng, -1), len(self.ops[eng]))
        self.ops[eng].append(op)
        for (name, p0, p1, b0, b1) in wregs:
            lst = self.recs[name]
            lst[:] = [r for r in lst if not (p0 <= r[3] and r[4] <= p1 and b0 <= r[5] and r[6] <= b1)]
            lst.append((mykey, myval, True, p0, p1, b0, b1))
        for (name, p0, p1, b0, b1) in rregs:
            lst = self.recs[name]
            lst[:] = [r for r in lst if not ((not r[2]) and r[0] == mykey and p0 <= r[3] and r[4] <= p1
                                             and b0 <= r[5] and r[6] <= b1)]
            lst.append((mykey, myval, False, p0, p1, b0, b1))
        return op

    def mm(self, out, lhsT, rhs, start=True, stop=True):
        self.add("pe", lambda e: e.matmul(out, lhsT=lhsT, rhs=rhs, start=start, stop=stop),
                 [lhsT, rhs], [out])

    def tr(self, out, in_, ident):
        self.add("pe", lambda e: e.transpose(out, in_, ident), [in_, ident], [out])

    def act(self, out, in_, func, bias=None, scale=None):
        kw = {}
        reads = [in_]
        if bias is not None:
            kw["bias"] = bias
            if not isinstance(bias, float):
                reads.append(bias)
        if scale is not None:
            kw["scale"] = scale
        self.add("act", lambda e: e.activation(out=out, in_=in_, func=func, **kw), reads, [out])

    def tt(self, out, in0, in1, op, eng="dve"):
        self.add(eng, lambda e: e.tensor_tensor(out=out, in0=in0, in1=in1, op=op), [in0, in1], [out])

    def ts(self, out, in0, s1, s2, op0, op1=None, eng="dve"):
        reads = [in0] + [s for s in (s1, s2) if s is not None and not isinstance(s, float)]
        if op1 is None:
            self.add(eng, lambda e: e.tensor_scalar(out=out, in0=in0, scalar1=s1, scalar2=None, op0=op0),
                     reads, [out])
        else:
            self.add(eng, lambda e: e.tensor_scalar(out=out, in0=in0, scalar1=s1, scalar2=s2, op0=op0, op1=op1),
                     reads, [out])

    def stt(self, out, in0, scalar, in1, op0, op1, eng="dve"):
        reads = [in0, in1] + ([] if isinstance(scalar, float) else [scalar])
        self.add(eng, lambda e: e.scalar_tensor_tensor(out=out, in0=in0, scalar=scalar, in1=in1, op0=op0, op1=op1),
                 reads, [out])

    def copy(self, out, in_, eng="dve"):
        if eng == "act":
            self.add("act", lambda e: e.activation(out=out, in_=in_, func=AF.Copy), [in_], [out])
        else:
            self.add(eng, lambda e: e.tensor_copy(out=out, in_=in_), [in_], [out])

    def recip(self, out, in_):
        self.add("dve", lambda e: e.reciprocal(out=out, in_=in_), [in_], [out])

    def memset(self, ap, val, eng="dve"):
        self.add(eng, lambda e: e.memset(ap, val), [], [ap])

    def dma(self, out, in_, q, is_out=False):
        self.add(q, lambda e: e.dma_start(out=out, in_=in_), [in_], [out], dma=True, is_out=is_out)

    def emit(self, es):
        nc = self.nc
        sems = {}
        for e in ("pe", "act", "dve", "pool"):
            sems[e] = es.enter_context(nc.semaphore("s_" + e))
        for q in ("pool", "sp"):
            for k in range(N_DMA_SEM[q]):
                sems[("dma", q, k)] = es.enter_context(nc.semaphore("d_%s_%d" % (q, k)))
        counts = {}
        for e in self.ENGS:
            c = 0
            lst = []
            for op in self.ops[e]:
                if op.sig:
                    c += 1
                lst.append(c)
            counts[e] = lst
        final_waits = list(self.out_tokens)

        def run(ename, e):
            for op in self.ops[ename]:
                ws = [(sems[key], val if isinstance(key, tuple) else counts[key][val]) for key, val in op.waits]
                emb = None
                if EMBED_WAIT and ws and op.dma is None:
                    emb = ws.pop()
                for sm, vv in ws:
                    e.wait_ge(sm, vv)
                ins = op.fn(e)
                if emb is not None:
                    ins._wait_ge(emb[0], emb[1])
                if op.dma is not None:
                    ins.then_inc(sems[op.dma[0]], 16)
                elif op.sig:
                    ins.then_inc(sems[ename], 1)
            if ename == "sp":
                for key, val in final_waits:
                    e.wait_ge(sems[key], val)

        block = es.enter_context(nc.Block())

        @block.tensor
        def _(e):
            run("pe", e)

        @block.scalar
        def _(e):
            run("act", e)

        @block.vector
        def _(e):
            run("dve", e)

        @block.gpsimd
        def _(e):
            run("pool", e)

        @block.sync
        def _(e):
            run("sp", e)


class Arena:
    def __init__(self, base_ap):
        self.base = base_ap

    def at(self, off, shape, dt):
        n = int(np.prod(shape)) * _esz(dt)
        assert off % 4 == 0 and off + n <= ARENA_BYTES, (off, n, ARENA_BYTES)
        n4 = (n + 3) // 4
        v = self.base[:, off // 4: off // 4 + n4]
        if dt == BF16:
            v = v.bitcast(BF16)
            v = v[:, 0:int(np.prod(shape))]
        if len(shape) == 1:
            return v
        names = " ".join("d%d" % i for i in range(len(shape)))
        kw = {"d%d" % i: int(shape[i]) for i in range(len(shape))}
        return v.rearrange("p (%s) -> p %s" % (names, names), **kw)


class Bump:
    def __init__(self, arena, start):
        self.arena = arena
        self.off = start

    def alloc(self, shape, dt):
        n = int(np.prod(shape)) * _esz(dt)
        o = self.off
        self.off = o + ((n + 31) // 32) * 32
        return self.arena.at(o, shape, dt)


def build_program(n_seq=SEQ_PER_CORE, n_layers=2, stop_after=None, dbg=None):
    nc = bass.Bass("TRN2", target_bir_lowering=False)
    P = Prog(nc)
    es = ExitStack()

    def dram(name, shape, kind="ExternalInput"):
        return nc.dram_tensor(name, list(shape), F32, kind=kind).ap()

    xT = dram("xT", [n_seq, D, S])
    outT = dram("outT", [n_seq, D, S], kind="ExternalOutput")
    w_g = {1: dram("ffn1_w_gate", [2, D, DFF]), 2: dram("ffn2_w_gate", [2, D, DFF])}
    w_u = {1: dram("ffn1_w_up", [2, D, DFF]), 2: dram("ffn2_w_up", [2, D, DFF])}
    w_d = {1: dram("ffn1_w_down", [2, DFF, D]), 2: dram("ffn2_w_down", [2, DFF, D])}
    w_in = dram("w_in", [2, D, 6160])
    w_out = dram("w_out", [2, 2048, D])
    pvec_d = dram("pvec", [2, 128, NV])
    cf_d = dram("cf", [128, 384])
    cb_d = dram("cb", [128, 640])
    am_d = dram("am", [128, 9 * 512])

    def sb(name, shape, dt):
        t = es.enter_context(nc.sbuf_tensor("sb_" + name, list(shape), dt))
        P.register("sb_" + name, int(np.prod(shape[1:])) * _esz(dt))
        return t

    x = sb("x", [128, NCH, S], F32)
    hT = sb("hT", [128, NCH, S], BF16)
    am = sb("am", [128, 9, 512], BF16)
    cf = sb("cf", [128, 384], F32)
    cb = sb("cb", [128, 640], BF16)
    pv = sb("pv", [128, 2, NV], F32)
    arena_t = sb("arena", [128, ARENA_BYTES // 4], F32)
    ps = []
    for i in range(8):
        t = es.enter_context(nc.psum_tensor("ps%d" % i, [128, 512], F32))
        P.register("ps%d" % i, 2048)
        ps.append(t)

    tri = cf[:, 0:128]
    su = cf[:, 128:256]
    ones_f = cf[:, 256:384]
    ident = cb[:, 0:128]
    ones_b = cb[:, 128:256]
    blk_b = cb[:, 256:384]
    su_b = cb[:, 384:512]
    tri_b = cb[:, 512:640]

    A = Arena(arena_t[:, :])
    nb = Bump(A, 0)
    sq = nb.alloc([2, 512], BF16)
    lnv = nb.alloc([512], F32)
    rstd = nb.alloc([2, 512], F32)
    COMMON_END = nb.off

    fb = Bump(A, COMMON_END)
    ffn_slots = []
    for i in range(2):
        ffn_slots.append((fb.alloc([8, 768], BF16), fb.alloc([8, 768], BF16), fb.alloc([6, 1024], BF16)))
    aTb = [fb.alloc([6, 512], BF16) for _ in range(2)]
    sgb = [fb.alloc([512], F32) for _ in range(2)]

    mb = Bump(A, COMMON_END)
    yT = mb.alloc([4, S], BF16)
    wo_slot = mb.alloc([4, 1024], BF16)
    wdt = mb.alloc([8, 16], BF16)
    dt_tmp = mb.alloc([256], F32)
    dt_tok = mb.alloc([256], F32)
    adt = mb.alloc([256], F32)
    decay_tok = mb.alloc([256], BF16)
    adt_b = mb.alloc([256], BF16)
    cd_bc = mb.alloc([256], F32)
    a_bc = mb.alloc([16], F32)
    MIX_COMMON_END = mb.off
    sbm = Bump(A, MIX_COMMON_END)
    wssd = (sbm.alloc([8, 256], BF16), sbm.alloc([8, 128], BF16), sbm.alloc([8, 128], BF16), sbm.alloc([8, 256], BF16))
    xs = sbm.alloc([4, 516], BF16)
    cdiag = sbm.alloc([4, 4, 128], BF16)
    acc = sbm.alloc([2, 512], F32)
    xc = sbm.alloc([2, 2, 512], F32)
    xcb = sbm.alloc([4, 512], BF16)
    zs = sbm.alloc([2, 2, 512], BF16)
    X_tok = sbm.alloc([3, 256], BF16)
    Xd_tok = sbm.alloc([3, 256], BF16)
    B_tok = sbm.alloc([3, 128], BF16)
    rhsA_b = sbm.alloc([2, 512], BF16)
    lmat_b = sbm.alloc([2, 512], BF16)
    Ebc_b = sbm.alloc([2, 512], BF16)
    CBm_b = sbm.alloc([2, 128], BF16)
    MT_b = sbm.alloc([2, 512], BF16)
    Cs_b = sbm.alloc([2, 512], BF16)
    Hst = sbm.alloc([256], F32)
    Hb = sbm.alloc([2, 256], BF16)
    t1 = acc
    ab = Bump(A, MIX_COMMON_END)
    watt = [tuple(ab.alloc([8, 128], BF16) for _ in range(3)) for _ in range(2)]
    att_bufs = [dict(qT=ab.alloc([2, S], BF16), kT=ab.alloc([S], BF16), Vaug=ab.alloc([16, 2, 128], BF16))
                for _ in range(2)]
    pT = ab.alloc([3, 512], BF16)
    lnz = ab.alloc([512], F32)
    rz = ab.alloc([512], F32)

    def wview(w2d):
        return w2d.rearrange("(kc p) n -> p kc n", p=128)

    units = []
    state = {"ffn_i": 0, "att_i": 0}

    def plan():
        for s in range(n_seq):
            for l in range(n_layers):
                for gi in range(len(FFN_GROUPS)):
                    units.append(("ffn", s, l, 1, gi))
                units.append(("dt", s, l))
                units.append(("ssd", s, l, 0))
                units.append(("ssd", s, l, 1))
                units.append(("wo", s, l, 0))
                units.append(("ssd", s, l, 2))
                units.append(("ssd", s, l, 3))
                units.append(("wo", s, l, 1))
                for c in range(4):
                    units.append(("att", s, l, c))
                units.append(("wo", s, l, 2))
                for c in range(4, 8):
                    units.append(("att", s, l, c))
                units.append(("wo", s, l, 3))
                for gi in range(len(FFN_GROUPS)):
                    units.append(("ffn", s, l, 2, gi))

    plan()
    loaded = {}
    wpos = {"next_load": 0, "next_get": 0}

    def load_unit(u):
        kind = u[0]
        if kind == "ffn":
            _, s, l, f, gi = u
            J0, J1 = FFN_GROUPS[gi]
            G = J1 - J0
            slot = ffn_slots[state["ffn_i"] % 2]
            state["ffn_i"] += 1
            Wg = slot[0][:, :, 0:G * 128]
            Wu = slot[1][:, :, 0:G * 128]
            Wd = slot[2][:, 0:G, :]
            P.dma(Wg, wview(w_g[f][l])[:, :, J0 * 128:J1 * 128], "pool")
            P.dma(Wu, wview(w_u[f][l])[:, :, J0 * 128:J1 * 128], "pool")
            P.dma(Wd, wview(w_d[f][l])[:, J0:J1, :], "pool")
            return (Wg, Wu, Wd, G)
        if kind == "dt":
            _, s, l = u
            P.dma(wdt, wview(w_in[l])[:, :, DT0:DT0 + 16], "pool")
            return wdt
        if kind == "ssd":
            _, s, l, g = u
            wv = wview(w_in[l])
            P.dma(wssd[0], wv[:, :, X0 + 256 * g: X0 + 256 * g + 256], "pool")
            P.dma(wssd[1], wv[:, :, B0 + 128 * g: B0 + 128 * g + 128], "pool")
            P.dma(wssd[2], wv[:, :, C0 + 128 * g: C0 + 128 * g + 128], "pool")
            P.dma(wssd[3], wv[:, :, Z0 + 256 * g: Z0 + 256 * g + 256], "pool")
            return wssd
        if kind == "att":
            _, s, l, c = u
            wv = wview(w_in[l])
            slot = watt[state["att_i"] % 2]
            state["att_i"] += 1
            P.dma(slot[0], wv[:, :, Q0 + 128 * c: Q0 + 128 * c + 128], "pool")
            P.dma(slot[1], wv[:, :, K0 + 128 * c: K0 + 128 * c + 128], "pool")
            P.dma(slot[2], wv[:, :, V0 + 128 * c: V0 + 128 * c + 128], "pool")
            return slot
        if kind == "wo":
            _, s, l, q4 = u
            P.dma(wo_slot, wview(w_out[l])[:, 4 * q4:4 * q4 + 4, :], "pool")
            return wo_slot
        raise ValueError(kind)

    def wget(u):
        return load_unit(u)

    dumped = set()

    def dump(name, ap):
        if dbg is None or name not in dbg or name in dumped:
            return
        dumped.add(name)
        shp = [int(v) for v in ap.shape]
        d = nc.dram_tensor("dbg_" + name, shp, F32, kind="ExternalOutput").ap()
        P.dma(d, ap, "pool", is_out=True)

    P.dma(cf[:, :], cf_d, "sp")
    P.dma(cb[:, :], cb_d, "pool")
    P.dma(am[:, :, :], am_d.rearrange("p (a b) -> p a b", a=9), "pool")
    P.dma(pv[:, :, :], pvec_d.rearrange("l p n -> p l n"), "sp")

    def pcol(l, c):
        return pv[:, l, c:c + 1]

    def norm_tile(l, base, T):
        tok = slice(T * 512, (T + 1) * 512)
        ss = ps[T % 2]
        for c in range(NCH):
            P.act(sq[:, c % 2, :], x[:, c, tok], AF.Square)
            P.mm(ss[:, :], ones_b, sq[:, c % 2, :], start=(c == 0), stop=(c == NCH - 1))
        P.act(lnv, ss[:, :], AF.Ln, bias=EPS, scale=1.0 / D)
        P.act(rstd[:, T % 2, :], lnv, AF.Exp, scale=-0.5)
        for c in range(NCH):
            P.stt(hT[:, c, tok], x[:, c, tok], pcol(l, base + c), rstd[:, T % 2, :], ALU.mult, ALU.mult)

    def norm_phase(l, base):
        for T in range(4):
            norm_tile(l, base, T)

    def ffn_phase(s, l, f, tail=None):
        for gi in range(len(FFN_GROUPS)):
            Wg, Wu, Wd, G = wget(("ffn", s, l, f, gi))
            last = gi == len(FFN_GROUPS) - 1
            for T in range(4):
                if last and tail is not None and T >= 1:
                    pending_norm = (tail[0], tail[1], T - 1)
                else:
                    pending_norm = None
                tok = slice(T * 512, (T + 1) * 512)
                aT = aTb[T % 2]
                for jj in range(G):
                    gp = ps[jj % 2]
                    up = ps[2 + jj % 2]
                    for kc in range(NCH):
                        P.mm(gp[:, :], Wg[:, kc, jj * 128:(jj + 1) * 128], hT[:, kc, tok], kc == 0, kc == NCH - 1)
                    for kc in range(NCH):
                        P.mm(up[:, :], Wu[:, kc, jj * 128:(jj + 1) * 128], hT[:, kc, tok], kc == 0, kc == NCH - 1)
                    P.act(sgb[jj % 2], gp[:, :], AF.Silu)
                    P.tt(aT[:, jj, :], up[:, :], sgb[jj % 2], ALU.mult)
                for dc in range(NCH):
                    yp = ps[4 + dc % 4]
                    for jj in range(G):
                        P.mm(yp[:, :], Wd[:, jj, dc * 128:(dc + 1) * 128], aT[:, jj, :], jj == 0, jj == G - 1)
                    P.stt(x[:, dc, tok], yp[:, :], 0.5, x[:, dc, tok], ALU.mult, ALU.add)
                if pending_norm is not None:
                    norm_tile(*pending_norm)
            if last and tail is not None:
                norm_tile(tail[0], tail[1], 3)

    def dt_phase(s, l):
        W = wget(("dt", s, l))
        dtp = ps[2]
        for tc in range(16):
            for kc in range(NCH):
                P.mm(dtp[:, tc * 16:(tc + 1) * 16], hT[:, kc, tc * 128:(tc + 1) * 128], W[:, kc, :], kc == 0, kc == NCH - 1)
        v3 = lambda a: a.rearrange("p (a b) -> p a b", a=16)
        bias_b = pv[:, l, PV_DTB:PV_DTB + 16].unsqueeze(1).broadcast_to([128, 16, 16])
        P.tt(v3(dt_tmp), v3(dtp[:, 0:256]), bias_b, ALU.add)
        P.act(dt_tmp, dt_tmp, AF.Exp)
        P.act(dt_tok, dt_tmp, AF.Ln, bias=1.0)
        P.act(a_bc, pv[:, l, PV_ALOG:PV_ALOG + 16], AF.Exp)
        P.stt(v3(adt), v3(dt_tok), -1.0, a_bc.unsqueeze(1).broadcast_to([128, 16, 16]), ALU.mult, ALU.mult)
        P.copy(adt_b, adt, eng="act")
        decp = ps[3]
        cdp = ps[4]
        for tc in range(16):
            P.mm(decp[:, tc * 16:(tc + 1) * 16], su, adt[:, tc * 16:(tc + 1) * 16])
            P.mm(cdp[:, tc * 16:(tc + 1) * 16], ones_f, adt[:, tc * 16:(tc + 1) * 16])
        P.act(decay_tok, decp[:, 0:256], AF.Exp)
        P.act(cd_bc, cdp[:, 0:256], AF.Exp)
        dump("dt_tok", dt_tok); dump("adt", adt); dump("decay_tok", decay_tok); dump("cd_bc", cd_bc)

    tpb = ps[2][:, :].bitcast(BF16)

    ssd_pref = {}

    def ssd_gen(s, l, g):
        W = ssd_pref.pop((s, l, g), None)
        if W is None:
            W = wget(("ssd", s, l, g))
        P.memset(Hst, 0.0)
        P.memset(Hb[:, 0, :], 0.0)
        P.memset(xs[:, :, 0:3], 0.0)
        hcur = 0
        cidx = [2 * g, 2 * g + 1, 8 + g, 12 + g]
        for ch in range(4):
            for k in range(4):
                P.ts(cdiag[:, ch, k, :], ident, pcol(l, PV_CW + k * 16 + cidx[ch]), None, ALU.mult)
        h4 = lambda a, n: a.rearrange("p (h n) -> p h n", h=4)
        def Pm(T, ch):
            tk = slice(T * 512, (T + 1) * 512)
            pp = ps[ch % 2]
            Wsub, col = ((W[0], 0), (W[0], 128), (W[1], 0), (W[2], 0), (W[3], 0), (W[3], 128))[ch]
            for kc in range(NCH):
                P.mm(pp[:, :], Wsub[:, kc, col:col + 128], hT[:, kc, tk], kc == 0, kc == NCH - 1)

        def Cv(T, ch):
            pp = ps[ch % 2]
            P.copy(xs[:, ch, 3:515], pp[:, :], eng="act")
            ci = cidx[ch]
            cps = ps[3 + ch % 2]
            for k in range(4):
                P.mm(cps[:, :], cdiag[:, ch, k, :], xs[:, ch, k:k + 512], k == 0, k == 3)
            P.copy(xs[:, ch, 0:3], xs[:, ch, 512:515])
            if ch < 2:
                P.act(xc[:, T % 2, ch, :], cps[:, :], AF.Silu, bias=pcol(l, PV_CB + ci))
            P.act(xcb[:, ch, :], cps[:, :], AF.Silu, bias=pcol(l, PV_CB + ci))

        def epi(T):
            tk = slice(T * 512, (T + 1) * 512)
            yps_ = (ps[6], ps[7])
            for cc in range(2):
                chunk = 2 * g + cc
                P.stt(t1[:, cc, :], xc[:, T % 2, cc, :], pcol(l, PV_DSKIP + chunk), yps_[cc][:, :], ALU.mult, ALU.add)
                P.tt(t1[:, cc, :], t1[:, cc, :], zs[:, T % 2, cc, :], ALU.mult)
                P.act(sq[:, cc, :], t1[:, cc, :], AF.Square)
                P.mm(ps[3][:, :], ones_b, sq[:, cc, :], cc == 0, cc == 1)
            P.act(lnv, ps[3][:, :], AF.Ln, bias=EPS, scale=1.0 / 256)
            P.act(rstd[:, 0, :], lnv, AF.Exp, scale=-0.5)
            for cc in range(2):
                chunk = 2 * g + cc
                P.stt(yT[:, chunk % 4, tk], t1[:, cc, :], pcol(l, PV_SSDN + chunk), rstd[:, 0, :], ALU.mult, ALU.mult)

        Pm(0, 0)
        Pm(0, 1)
        for T in range(4):
            tok = slice(T * 512, (T + 1) * 512)
            for fn_, a_ in ((Cv, 0), (Pm, 2), (Cv, 1), (Pm, 3), (Cv, 2), (Pm, 4), (Cv, 3), (Pm, 5)):
                fn_(T, a_)
                yield
            if T == 3 and g + 1 < 4:
                ssd_pref[(s, l, g + 1)] = wget(("ssd", s, l, g + 1))
            P.act(zs[:, T % 2, 0, :], ps[0][:, :], AF.Silu)
            P.act(zs[:, T % 2, 1, :], ps[1][:, :], AF.Silu)
            yps = (ps[6], ps[7])
            if T >= 1:
                epi(T - 1)
            hbox = [hcur]

            def stA(tc4):
                c0 = tc4 * 128
                tcg = 4 * T + tc4
                tpo = (tc4 % 2) * 512
                for ch in range(3):
                    P.tr(tpb[:, tpo + ch * 128: tpo + (ch + 1) * 128], xcb[:, ch, c0:c0 + 128], ident)
                hsl = slice(tcg * 16 + 4 * g, tcg * 16 + 4 * g + 4)
                Xt = X_tok[:, tc4 % 3, :]
                Xd = Xd_tok[:, tc4 % 3, :]
                P.tt(h4(Xt, 64), h4(tpb[:, tpo:tpo + 256], 64),
                     dt_tok[:, hsl].unsqueeze(2).broadcast_to([128, 4, 64]), ALU.mult)
                P.copy(B_tok[:, tc4 % 3, :], tpb[:, tpo + 256:tpo + 384], eng="act")
                P.tt(h4(rhsA_b[:, tc4 % 2, :], 128), tri_b.unsqueeze(1).broadcast_to([128, 4, 128]),
                     adt_b[:, hsl].unsqueeze(2).broadcast_to([128, 4, 128]), ALU.mult)
                P.tt(h4(Xd, 64), h4(Xt, 64), decay_tok[:, hsl].unsqueeze(2).broadcast_to([128, 4, 64]), ALU.mult)

            def stB(tc4):
                c0 = tc4 * 128
                rhsA = rhsA_b[:, tc4 % 2, :]
                lmat = lmat_b[:, tc4 % 2, :]
                Ebc = Ebc_b[:, tc4 % 2, :]
                CBm = CBm_b[:, tc4 % 2, :]
                P.mm(ps[3][:, :], su_b, rhsA)
                P.mm(ps[4][:, :], ones_b, rhsA)
                P.mm(ps[5][:, 0:128], xcb[:, 2, c0:c0 + 128], xcb[:, 3, c0:c0 + 128])
                P.act(lmat, ps[3][:, :], AF.Exp)
                P.act(Ebc, ps[4][:, :], AF.Exp)
                P.tt(CBm, ps[5][:, 0:128], tri, ALU.mult)
                P.tt(h4(MT_b[:, tc4 % 2, :], 128), h4(lmat, 128), CBm.unsqueeze(1).broadcast_to([128, 4, 128]), ALU.mult)
                P.tt(h4(Cs_b[:, tc4 % 2, :], 128), h4(Ebc, 128),
                     xcb[:, 3, c0:c0 + 128].unsqueeze(1).broadcast_to([128, 4, 128]), ALU.mult)

            def stC(tc4):
                c0 = tc4 * 128
                tcg = 4 * T + tc4
                hsl = slice(tcg * 16 + 4 * g, tcg * 16 + 4 * g + 4)
                Xt = X_tok[:, tc4 % 3, :]
                Xd = Xd_tok[:, tc4 % 3, :]
                MT = MT_b[:, tc4 % 2, :]
                Cs = Cs_b[:, tc4 % 2, :]
                hc = hbox[0]
                P.mm(ps[5][:, 128:384], B_tok[:, tc4 % 3, :], Xd)
                for h in range(4):
                    cdcol = cd_bc[:, tcg * 16 + 4 * g + h: tcg * 16 + 4 * g + h + 1]
                    P.stt(Hst[:, h * 64:(h + 1) * 64], Hst[:, h * 64:(h + 1) * 64], cdcol,
                          ps[5][:, 128 + h * 64:128 + (h + 1) * 64], ALU.mult, ALU.add)
                P.copy(Hb[:, 1 - hc, :], Hst, eng="act")
                for h in range(4):
                    cc, hh = h // 2, h % 2
                    yo = yps[cc][hh * 64:(hh + 1) * 64, c0:c0 + 128]
                    P.mm(yo, Xt[:, h * 64:(h + 1) * 64], MT[:, h * 128:(h + 1) * 128], True, False)
                    P.mm(yo, Hb[:, hc, h * 64:(h + 1) * 64], Cs[:, h * 128:(h + 1) * 128], False, True)
                hbox[0] = 1 - hc

            nxt = []
            if T < 3:
                tkn = slice((T + 1) * 512, (T + 2) * 512)
                for ch_ in range(2):
                    Wsub_, col_ = ((W[0], 0), (W[0], 128))[ch_]
                    for kc in range(NCH):
                        nxt.append((ps[ch_], Wsub_[:, kc, col_:col_ + 128], hT[:, kc, tkn], kc == 0, kc == NCH - 1))
            for si, (st, i) in enumerate((("A", 0), ("A", 1), ("B", 0), ("A", 2), ("B", 1), ("C", 0), ("A", 3), ("B", 2),
                                          ("C", 1), ("B", 3), ("C", 2), ("C", 3))):
                {"A": stA, "B": stB, "C": stC}[st](i)
                if si >= 4:
                    for _ in range(2):
                        if nxt:
                            o_, l_, r_, st_, sp_ = nxt.pop(0)
                            P.mm(o_[:, :], l_, r_, st_, sp_)
                yield
            for o_, l_, r_, st_, sp_ in nxt:
                P.mm(o_[:, :], l_, r_, st_, sp_)
            hcur = hbox[0]
        epi(3)

    def dump_yT(tag):
        dump("yT_" + tag, yT[:, :, :])

    def att_proj_gen(s, l, c, B):
        qT, kT, Vaug = B["qT"], B["kT"], B["Vaug"]
        W = wget(("att", s, l, c))
        psv = ps[0][:, :].rearrange("p (a h e) -> p a h e", a=4, h=2)
        for T in range(4):
            tok = slice(T * 512, (T + 1) * 512)
            for kc in range(NCH):
                P.mm(ps[0][:, :], W[0][:, kc, :], hT[:, kc, tok], kc == 0, kc == NCH - 1)
            yield
            P.act(sq[:, 0, :], ps[0][:, :], AF.Square)
            for kc in range(NCH):
                P.mm(ps[1][:, :], W[1][:, kc, :], hT[:, kc, tok], kc == 0, kc == NCH - 1)
            yield
            P.act(sq[:, 1, :], ps[1][:, :], AF.Square)
            P.mm(ps[2][:, :], blk_b, sq[:, 0, :])
            yield
            P.act(lnv, ps[2][:, :], AF.Ln, bias=EPS, scale=1.0 / 64)
            P.act(rstd[:, 0, :], lnv, AF.Exp, scale=-0.5)
            yield
            P.mm(ps[2][:, :], blk_b, sq[:, 1, :])
            P.stt(qT[0:64, 0, tok], ps[0][0:64, :], pv[0:64, l, PV_QN:PV_QN + 1], rstd[0:64, 0, :], ALU.mult, ALU.mult)
            P.stt(qT[64:128, 1, tok], ps[0][64:128, :], pv[64:128, l, PV_QN:PV_QN + 1], rstd[64:128, 0, :],
                  ALU.mult, ALU.mult)
            yield
            P.act(lnv, ps[2][:, :], AF.Ln, bias=EPS, scale=1.0 / 64)
            P.act(rstd[:, 1, :], lnv, AF.Exp, scale=-0.5)
            yield
            P.stt(kT[:, tok], ps[1][:, :], pcol(l, PV_KN), rstd[:, 1, :], ALU.mult, ALU.mult)
            for kb4 in range(4):
                kb = 4 * T + kb4
                for kc in range(NCH):
                    P.mm(ps[0][:, kb4 * 128:(kb4 + 1) * 128], hT[:, kc, kb * 128:(kb + 1) * 128], W[2][:, kc, :],
                         kc == 0, kc == NCH - 1)
            yield
            P.copy(Vaug[:, 4 * T:4 * T + 4, 0, 0:64], psv[:, :, 0, :])
            P.copy(Vaug[:, 4 * T:4 * T + 4, 1, 64:128], psv[:, :, 1, :])
            yield

    def att_blocks_gen(s, l, c, B):
        qT, kT, Vaug = B["qT"], B["kT"], B["Vaug"]
        blocks = []
        for hh in range(2):
            for T in range(4):
                nj = 4 * T + 4
                for j in range(nj):
                    blocks.append((hh, T, j, nj))
        sbank = (ps[3], ps[4], ps[5])

        def cols(T, j):
            r = j - 4 * T
            return 128 * r if r > 0 else 0

        def qk(i):
            hh, T, j, nj = blocks[i]
            Up = slice(hh * 64, hh * 64 + 64)
            q0 = cols(T, j)
            P.mm(sbank[i % 3][:, q0:512], kT[:, j * 128:(j + 1) * 128], qT[:, hh, T * 512 + q0:(T + 1) * 512])

        LOOK = 2
        pending = []
        for i in range(min(LOOK, len(blocks))):
            qk(i)
        for i, (hh, T, j, nj) in enumerate(blocks):
            if i + LOOK < len(blocks):
                qk(i + LOOK)
            Up = slice(hh * 64, hh * 64 + 64)
            Zp = slice((1 - hh) * 64, (1 - hh) * 64 + 64)
            tok = slice(T * 512, (T + 1) * 512)
            q0 = cols(T, j)
            op_ = ps[6 + (T % 2)]
            pt = pT[:, i % 3, :]
            P.act(pt[:, q0:512], sbank[i % 3][:, q0:512], AF.Exp, scale=0.125)
            m = 4 * T - j
            mi = m + 3 if m <= 4 else 8
            P.tt(pt[:, q0:512], pt[:, q0:512], am[:, mi, q0:512], ALU.mult)
            P.mm(op_[:, q0:512], Vaug[:, j, hh, :], pt[:, q0:512], j == 0, j == nj - 1)
            if j == nj - 1:
                pending.append((i + 3, Up, Zp, tok, op_))
            while pending and pending[0][0] <= i:
                _, Up_, Zp_, tok_, o_ = pending.pop(0)
                P.act(lnz[Up_, :], o_[Zp_, :], AF.Ln)
                P.act(rz[Up_, :], lnz[Up_, :], AF.Exp, scale=-1.0)
                P.tt(yT[Up_, c % 4, tok_], o_[Up_, :], rz[Up_, :], ALU.mult)
            yield
        for _, Up_, Zp_, tok_, o_ in pending:
            P.act(lnz[Up_, :], o_[Zp_, :], AF.Ln)
            P.act(rz[Up_, :], lnz[Up_, :], AF.Exp, scale=-1.0)
            P.tt(yT[Up_, c % 4, tok_], o_[Up_, :], rz[Up_, :], ALU.mult)

    def interleave(main, sides):
        sides = [[g, r, 0.0, False] for (g, r) in sides if g is not None]
        for _ in main:
            for sd in sides:
                if sd[3]:
                    continue
                sd[2] += sd[1]
                while sd[2] >= 1.0 and not sd[3]:
                    sd[2] -= 1.0
                    try:
                        next(sd[0])
                    except StopIteration:
                        sd[3] = True
        for sd in sides:
            if not sd[3]:
                for _ in sd[0]:
                    pass

    def att_all(s, l, tail=None):
        for B in att_bufs:
            P.memset(B["Vaug"][:, :, 0, 64:128], 1.0)
            P.memset(B["Vaug"][:, :, 1, 0:64], 1.0)
            P.memset(B["qT"][64:128, 0, :], 0.0)
            P.memset(B["qT"][0:64, 1, :], 0.0)
        run(att_proj_gen(s, l, 0, att_bufs[0]))
        for c in range(8):
            sides = []
            if c + 1 < 8:
                sides.append((att_proj_gen(s, l, c + 1, att_bufs[(c + 1) % 2]), 0.5))
            interleave(att_blocks_gen(s, l, c, att_bufs[c % 2]), sides)
            if c == 3:
                out_proj(s, l, 2)
        out_proj(s, l, 3, tail)

    def out_proj_gen(s, l, q4, tail=None, banks=(0, 1)):
        Wo = wget(("wo", s, l, q4))
        for T in range(4):
            tok = slice(T * 512, (T + 1) * 512)
            for dc in range(NCH):
                yp = ps[banks[dc % 2]]
                for kc in range(4):
                    P.mm(yp[:, :], Wo[:, kc, dc * 128:(dc + 1) * 128], yT[:, kc, tok], kc == 0, kc == 3)
                P.tt(x[:, dc, tok], yp[:, :], x[:, dc, tok], ALU.add)
                yield
            if tail is not None and T >= 1:
                norm_tile(tail[0], tail[1], T - 1)
        if tail is not None:
            norm_tile(tail[0], tail[1], 3)

    def run(gen):
        for _ in gen:
            pass

    def out_proj(s, l, q4, tail=None):
        run(out_proj_gen(s, l, q4, tail))

    nphase = [0]

    def go():
        nphase[0] += 1
        return stop_after is None or nphase[0] <= stop_after

    fuse = (stop_after is None) or FORCE_FUSE

    def layers(s):
        for l in range(n_layers):
            if not go():
                return
            if l == 0 or not fuse:
                norm_phase(l, PV_FFN1)
            ffn_phase(s, l, 1, tail=(l, PV_MIX) if fuse else None)
            if not go():
                return
            if not fuse:
                norm_phase(l, PV_MIX)
            dt_phase(s, l)
            run(ssd_gen(s, l, 0))
            run(ssd_gen(s, l, 1))
            dump_yT("ssd01")
            out_proj(s, l, 0)
            run(ssd_gen(s, l, 2))
            run(ssd_gen(s, l, 3))
            out_proj(s, l, 1)
            if not go():
                return
            if fuse:
                att_all(s, l, tail=(l, PV_FFN2))
            else:
                att_all_plain(s, l)
            if not go():
                return
            if not fuse:
                norm_phase(l, PV_FFN2)
            ffn_phase(s, l, 2, tail=(l + 1, PV_FFN1) if (fuse and l + 1 < n_layers) else None)

    def att_all_plain(s, l):
        for B in att_bufs:
            P.memset(B["Vaug"][:, :, 0, 64:128], 1.0)
            P.memset(B["Vaug"][:, :, 1, 0:64], 1.0)
            P.memset(B["qT"][64:128, 0, :], 0.0)
            P.memset(B["qT"][0:64, 1, :], 0.0)
        run(att_proj_gen(s, l, 0, att_bufs[0]))
        for c in range(8):
            sides = []
            if c + 1 < 8:
                sides.append((att_proj_gen(s, l, c + 1, att_bufs[(c + 1) % 2]), 0.5))
            interleave(att_blocks_gen(s, l, c, att_bufs[c % 2]), sides)
            if c == 3:
                out_proj(s, l, 2)
        out_proj(s, l, 3)

    for s in range(n_seq):
        for c in range(NCH):
            P.dma(x[:, c, :], xT[s, c * 128:(c + 1) * 128, :], "sp")
        layers(s)
        for c in range(NCH):
            P.dma(outT[s, c * 128:(c + 1) * 128, :], x[:, c, :], "sp", is_out=True)
        if stop_after is not None:
            break
    P.emit(es)
    es.close()
    return nc


def make_consts():
    k = np.arange(128)
    tri = (k[:, None] <= k[None, :]).astype(np.float32)
    su = (k[:, None] > k[None, :]).astype(np.float32)
    ones = np.ones((128, 128), np.float32)
    cf = np.concatenate([tri, su, ones], axis=1)
    ident = np.eye(128, dtype=np.float32)
    blk = np.zeros((128, 128), np.float32)
    blk[:64, :64] = 1.0
    blk[64:, 64:] = 1.0
    cb = np.concatenate([ident, ones, blk, su, tri], axis=1)
    am = np.zeros((128, 9, 512), np.float32)
    kk = np.arange(128)[:, None]
    qq = np.arange(512)[None, :]
    for mi in range(8):
        m = mi - 3
        d = 128 * m + qq - kk
        c = ((d >= 0) & (d <= 128)).astype(np.float32)
        c += ((d >= 0) & (d % 4 == 0) & (d <= 512)).astype(np.float32)
        c += ((d >= 0) & (d % 16 == 0)).astype(np.float32)
        am[:, mi, :] = c
    am[:, 8, :] = (((qq - kk) % 16) == 0).astype(np.float32)
    return cf, cb, np.ascontiguousarray(am.reshape(128, 9 * 512))


def make_pvec(inp):
    pv = np.zeros((2, 128, NV), np.float32)
    p = np.arange(128)
    for l in range(2):
        def cols(v, n):
            return np.asarray(v, np.float32).reshape(n, 128).T
        pv[l, :, PV_FFN1:PV_FFN1 + 8] = cols(inp["ffn1_norm"][l], 8)
        pv[l, :, PV_MIX:PV_MIX + 8] = cols(inp["mix_norm"][l], 8)
        pv[l, :, PV_FFN2:PV_FFN2 + 8] = cols(inp["ffn2_norm"][l], 8)
        pv[l, :, PV_SSDN:PV_SSDN + 8] = cols(inp["ssd_norm"][l], 8)
        ds = np.asarray(inp["d_skip"][l], np.float32)
        for c in range(8):
            pv[l, :, PV_DSKIP + c] = ds[2 * c + p // 64]
        pv[l, :, PV_QN] = np.asarray(inp["q_norm"][l], np.float32)[p % 64]
        pv[l, :, PV_KN] = np.asarray(inp["k_norm"][l], np.float32)[p % 64]
        pv[l, :, PV_CB:PV_CB + 16] = cols(inp["conv_b"][l], 16)
        cw = np.asarray(inp["conv_w"][l], np.float32)
        for k in range(4):
            pv[l, :, PV_CW + k * 16:PV_CW + (k + 1) * 16] = cols(cw[k], 16)
        pv[l, :, PV_DTB:PV_DTB + 16] = np.asarray(inp["dt_bias"][l], np.float32)[None, :]
        pv[l, :, PV_ALOG:PV_ALOG + 16] = np.asarray(inp["a_log"][l], np.float32)[None, :]
    return pv


_W_NAMES = ["ffn1_w_gate", "ffn1_w_up", "ffn1_w_down", "ffn2_w_gate", "ffn2_w_up", "ffn2_w_down", "w_in", "w_out"]


def kernel(**inputs):
    x = np.asarray(inputs["x"], np.float32)
    nb = x.shape[0]
    assert nb == N_CORES * SEQ_PER_CORE
    cf, cb, am = make_consts()
    pvec = make_pvec(inputs)
    shared = {n: np.ascontiguousarray(np.asarray(inputs[n], np.float32)) for n in _W_NAMES}
    shared.update({"pvec": pvec, "cf": cf, "cb": cb, "am": am})
    in_maps = []
    for core in range(N_CORES):
        xs = x[core * SEQ_PER_CORE:(core + 1) * SEQ_PER_CORE]
        m = dict(shared)
        m["xT"] = np.ascontiguousarray(xs.transpose(0, 2, 1))
        in_maps.append(m)
    nc = build_program()
    res = run_bass_kernel_spmd(nc, in_maps, core_ids=list(range(N_CORES)))
    out = np.empty((nb, S, D), np.float32)
    for core in range(N_CORES):
        o = np.asarray(res.results[core]["outT"])
        out[core * SEQ_PER_CORE:(core + 1) * SEQ_PER_CORE] = o.transpose(0, 2, 1)
    return out
```

```python
import numpy as np
from contextlib import ExitStack
import concourse.bass as bass
import concourse.mybir as mybir
from concourse.bass_utils import run_bass_kernel_spmd

F32 = mybir.dt.float32
BF16 = mybir.dt.bfloat16
AF = mybir.ActivationFunctionType
ALU = mybir.AluOpType

S = 2048
D = 1024
DFF = 2816
NCH = 8
EPS = 1e-6
N_CORES = 8
SEQ_PER_CORE = 2

Z0, X0, B0, C0, DT0, Q0, K0, V0 = 0, 1024, 2048, 2560, 3072, 3088, 4112, 5136

PV_FFN1, PV_MIX, PV_FFN2, PV_SSDN, PV_DSKIP, PV_QN, PV_KN, PV_CB, PV_CW, PV_DTB, PV_ALOG, NV = (
    0, 8, 16, 24, 32, 40, 41, 42, 58, 122, 138, 154)

ARENA_BYTES = 97 * 1024
FFN_GROUPS = [(0, 6), (6, 12), (12, 17), (17, 22)]
N_DMA_SEM = {"pool": 24, "sp": 20}
FORCE_FUSE = False
EMBED_WAIT = True


def _esz(dt):
    return 4 if dt == F32 else 2


class _Op:
    __slots__ = ("eng", "fn", "waits", "sig", "dma", "vc")

    def __init__(self, eng, fn):
        self.eng = eng
        self.fn = fn
        self.waits = []
        self.sig = False
        self.dma = None


class Prog:
    ENGS = ("pe", "act", "dve", "pool", "sp")

    def __init__(self, nc):
        self.nc = nc
        self.ops = {e: [] for e in self.ENGS}
        self.recs = {}
        self.tinfo = {}
        self.waited = {e: {} for e in self.ENGS}
        self.dma_rr = {"pool": 0, "sp": 0}
        self.dma_cnt = {}
        self.out_tokens = []
        self.evc = {e: {} for e in self.ENGS}
        self.dma_vc = {}

    def register(self, name, pstride_bytes):
        self.tinfo[name] = pstride_bytes
        self.recs[name] = []

    def region(self, ap):
        name = ap.tensor.name
        pb = self.tinfo.get(name)
        if pb is None:
            return None
        esz = _esz(ap.dtype)
        pstride = pb // esz
        off = ap.offset
        dims = ap.ap
        p0 = off // pstride
        fo = off - p0 * pstride
        ext = 0
        for st, cn in dims[1:]:
            ext += (cn - 1) * abs(st)
        if name.startswith("ps"):
            return (name, 0, 128, 0, 2048)
        return (name, p0, p0 + dims[0][1], fo * esz, (fo + ext + 1) * esz)

    def add(self, eng, fn, reads, writes, dma=False, is_out=False):
        op = _Op(eng, fn)
        deps = {}
        rregs = [r for r in (self.region(a) for a in reads) if r is not None]
        wregs = [r for r in (self.region(a) for a in writes) if r is not None]
        for (name, p0, p1, b0, b1) in rregs:
            isps = name.startswith("ps")
            for r in self.recs[name]:
                if (r[2] or (isps and r[0] != eng)) and r[3] < p1 and p0 < r[4] and r[5] < b1 and b0 < r[6]:
                    if r[0] == "pe" and eng == "pe":
                        continue
                    if deps.get(r[0], -1) < r[1]:
                        deps[r[0]] = r[1]
        for (name, p0, p1, b0, b1) in wregs:
            for r in self.recs[name]:
                if r[3] < p1 and p0 < r[4] and r[5] < b1 and b0 < r[6]:
                    if r[0] == eng and eng == "pe":
                        continue
                    if deps.get(r[0], -1) < r[1]:
                        deps[r[0]] = r[1]
        if dma:
            k = self.dma_rr[eng]
            self.dma_rr[eng] = (k + 1) % N_DMA_SEM[eng]
            key = ("dma", eng, k)
            m = self.dma_cnt.get(key, 0) + 1
            self.dma_cnt[key] = m
            if m > 1:
                if deps.get(key, -1) < 16 * (m - 1):
                    deps[key] = 16 * (m - 1)
            op.dma = (key, 16 * m)
            mykey, myval = key, 16 * m
            if is_out:
                self.out_tokens.append((key, 16 * m))
        else:
            mykey, myval = eng, len(self.ops[eng])
        w = self.evc[eng]
        for key, val in sorted(deps.items(), key=lambda kv: str(kv[0])):
            if w.get(key, -1) >= val:
                continue
            op.waits.append((key, val))
            if isinstance(key, tuple):
                src = self.dma_vc.get((key, val), {})
            else:
                self.ops[key][val].sig = True
                src = self.ops[key][val].vc
            for k2, v2 in src.items():
                if w.get(k2, -1) < v2:
                    w[k2] = v2
            if w.get(key, -1) < val:
                w[key] = val
        if dma:
            self.dma_vc[(mykey, myval)] = dict(w)
            op.vc = None
        else:
            op.vc = dict(w)
            op.vc[eng] = max(op.vc.get(eng, -1), len(self.ops[eng]))
        self.ops[eng].append(op)
        for (name, p0, p1, b0, b1) in wregs:
            lst = self.recs[name]
            lst[:] = [r for r in lst if not (p0 <= r[3] and r[4] <= p1 and b0 <= r[5] and r[6] <= b1)]
            lst.append((mykey, myval, True, p0, p1, b0, b1))
        for (name, p0, p1, b0, b1) in rregs:
            lst = self.recs[name]
            lst[:] = [r for r in lst if not ((not r[2]) and r[0] == mykey and p0 <= r[3] and r[4] <= p1
                                             and b0 <= r[5] and r[6] <= b1)]
            lst.append((mykey, myval, False, p0, p1, b0, b1))
        return op

    def mm(self, out, lhsT, rhs, start=True, stop=True):
        self.add("pe", lambda e: e.matmul(out, lhsT=lhsT, rhs=rhs, start=start, stop=stop),
                 [lhsT, rhs], [out])

    def tr(self, out, in_, ident):
        self.add("pe", lambda e: e.transpose(out, in_, ident), [in_, ident], [out])

    def act(self, out, in_, func, bias=None, scale=None):
        kw = {}
        reads = [in_]
        if bias is not None:
            kw["bias"] = bias
            if not isinstance(bias, float):
                reads.append(bias)
        if scale is not None:
            kw["scale"] = scale
        self.add("act", lambda e: e.activation(out=out, in_=in_, func=func, **kw), reads, [out])

    def tt(self, out, in0, in1, op, eng="dve"):
        self.add(eng, lambda e: e.tensor_tensor(out=out, in0=in0, in1=in1, op=op), [in0, in1], [out])

    def ts(self, out, in0, s1, s2, op0, op1=None, eng="dve"):
        reads = [in0] + [s for s in (s1, s2) if s is not None and not isinstance(s, float)]
        if op1 is None:
            self.add(eng, lambda e: e.tensor_scalar(out=out, in0=in0, scalar1=s1, scalar2=None, op0=op0),
                     reads, [out])
        else:
            self.add(eng, lambda e: e.tensor_scalar(out=out, in0=in0, scalar1=s1, scalar2=s2, op0=op0, op1=op1),
                     reads, [out])

    def stt(self, out, in0, scalar, in1, op0, op1, eng="dve"):
        reads = [in0, in1] + ([] if isinstance(scalar, float) else [scalar])
        self.add(eng, lambda e: e.scalar_tensor_tensor(out=out, in0=in0, scalar=scalar, in1=in1, op0=op0, op1=op1),
                 reads, [out])

    def copy(self, out, in_, eng="dve"):
        if eng == "act":
            self.add("act", lambda e: e.activation(out=out, in_=in_, func=AF.Copy), [in_], [out])
        else:
            self.add(eng, lambda e: e.tensor_copy(out=out, in_=in_), [in_], [out])

    def recip(self, out, in_):
        self.add("dve", lambda e: e.reciprocal(out=out, in_=in_), [in_], [out])

    def memset(self, ap, val, eng="dve"):
        self.add(eng, lambda e: e.memset(ap, val), [], [ap])

    def dma(self, out, in_, q, is_out=False):
        self.add(q, lambda e: e.dma_start(out=out, in_=in_), [in_], [out], dma=True, is_out=is_out)

    def emit(self, es):
        nc = self.nc
        sems = {}
        for e in ("pe", "act", "dve", "pool"):
            sems[e] = es.enter_context(nc.semaphore("s_" + e))
        for q in ("pool", "sp"):
            for k in range(N_DMA_SEM[q]):
                sems[("dma", q, k)] = es.enter_context(nc.semaphore("d_%s_%d" % (q, k)))
        counts = {}
        for e in self.ENGS:
            c = 0
            lst = []
            for op in self.ops[e]:
                if op.sig:
                    c += 1
                lst.append(c)
            counts[e] = lst
        final_waits = list(self.out_tokens)

        def run(ename, e):
            for op in self.ops[ename]:
                ws = [(sems[key], val if isinstance(key, tuple) else counts[key][val]) for key, val in op.waits]
                emb = None
                if EMBED_WAIT and ws and op.dma is None:
                    emb = ws.pop()
                for sm, vv in ws:
                    e.wait_ge(sm, vv)
                ins = op.fn(e)
                if emb is not None:
                    ins._wait_ge(emb[0], emb[1])
                if op.dma is not None:
                    ins.then_inc(sems[op.dma[0]], 16)
                elif op.sig:
                    ins.then_inc(sems[ename], 1)
            if ename == "sp":
                for key, val in final_waits:
                    e.wait_ge(sems[key], val)

        block = es.enter_context(nc.Block())

        @block.tensor
        def _(e):
            run("pe", e)

        @block.scalar
        def _(e):
            run("act", e)

        @block.vector
        def _(e):
            run("dve", e)

        @block.gpsimd
        def _(e):
            run("pool", e)

        @block.sync
        def _(e):
            run("sp", e)


class Arena:
    def __init__(self, base_ap):
        self.base = base_ap

    def at(self, off, shape, dt):
        n = int(np.prod(shape)) * _esz(dt)
        assert off % 4 == 0 and off + n <= ARENA_BYTES, (off, n, ARENA_BYTES)
        n4 = (n + 3) // 4
        v = self.base[:, off // 4: off // 4 + n4]
        if dt == BF16:
            v = v.bitcast(BF16)
            v = v[:, 0:int(np.prod(shape))]
        if len(shape) == 1:
            return v
        names = " ".join("d%d" % i for i in range(len(shape)))
        kw = {"d%d" % i: int(shape[i]) for i in range(len(shape))}
        return v.rearrange("p (%s) -> p %s" % (names, names), **kw)


class Bump:
    def __init__(self, arena, start):
        self.arena = arena
        self.off = start

    def alloc(self, shape, dt):
        n = int(np.prod(shape)) * _esz(dt)
        o = self.off
        self.off = o + ((n + 31) // 32) * 32
        return self.arena.at(o, shape, dt)


def build_program(n_seq=SEQ_PER_CORE, n_layers=2, stop_after=None, dbg=None):
    nc = bass.Bass("TRN2", target_bir_lowering=False)
    P = Prog(nc)
    es = ExitStack()

    def dram(name, shape, kind="ExternalInput"):
        return nc.dram_tensor(name, list(shape), F32, kind=kind).ap()

    xT = dram("xT", [n_seq, D, S])
    outT = dram("outT", [n_seq, D, S], kind="ExternalOutput")
    w_g = {1: dram("ffn1_w_gate", [2, D, DFF]), 2: dram("ffn2_w_gate", [2, D, DFF])}
    w_u = {1: dram("ffn1_w_up", [2, D, DFF]), 2: dram("ffn2_w_up", [2, D, DFF])}
    w_d = {1: dram("ffn1_w_down", [2, DFF, D]), 2: dram("ffn2_w_down", [2, DFF, D])}
    w_in = dram("w_in", [2, D, 6160])
    w_out = dram("w_out", [2, 2048, D])
    pvec_d = dram("pvec", [2, 128, NV])
    cf_d = dram("cf", [128, 384])
    cb_d = dram("cb", [128, 640])
    am_d = dram("am", [128, 9 * 512])

    def sb(name, shape, dt):
        t = es.enter_context(nc.sbuf_tensor("sb_" + name, list(shape), dt))
        P.register("sb_" + name, int(np.prod(shape[1:])) * _esz(dt))
        return t

    x = sb("x", [128, NCH, S], F32)
    hT = sb("hT", [128, NCH, S], BF16)
    am = sb("am", [128, 9, 512], BF16)
    cf = sb("cf", [128, 384], F32)
    cb = sb("cb", [128, 640], BF16)
    pv = sb("pv", [128, 2, NV], F32)
    arena_t = sb("arena", [128, ARENA_BYTES // 4], F32)
    ps = []
    for i in range(8):
        t = es.enter_context(nc.psum_tensor("ps%d" % i, [128, 512], F32))
        P.register("ps%d" % i, 2048)
        ps.append(t)

    tri = cf[:, 0:128]
    su = cf[:, 128:256]
    ones_f = cf[:, 256:384]
    ident = cb[:, 0:128]
    ones_b = cb[:, 128:256]
    blk_b = cb[:, 256:384]
    su_b = cb[:, 384:512]
    tri_b = cb[:, 512:640]

    A = Arena(arena_t[:, :])
    nb = Bump(A, 0)
    sq = nb.alloc([2, 512], BF16)
    lnv = nb.alloc([512], F32)
    rstd = nb.alloc([2, 512], F32)
    COMMON_END = nb.off

    fb = Bump(A, COMMON_END)
    ffn_slots = []
    for i in range(2):
        ffn_slots.append((fb.alloc([8, 768], BF16), fb.alloc([8, 768], BF16), fb.alloc([6, 1024], BF16)))
    aTb = [fb.alloc([6, 512], BF16) for _ in range(2)]
    sgb = [fb.alloc([512], F32) for _ in range(2)]

    mb = Bump(A, COMMON_END)
    yT = mb.alloc([4, S], BF16)
    wo_slot = mb.alloc([4, 1024], BF16)
    wdt = mb.alloc([8, 16], BF16)
    dt_tmp = mb.alloc([256], F32)
    dt_tok = mb.alloc([256], F32)
    adt = mb.alloc([256], F32)
    decay_tok = mb.alloc([256], BF16)
    adt_b = mb.alloc([256], BF16)
    cd_bc = mb.alloc([256], F32)
    a_bc = mb.alloc([16], F32)
    MIX_COMMON_END = mb.off
    sbm = Bump(A, MIX_COMMON_END)
    wssd = (sbm.alloc([8, 256], BF16), sbm.alloc([8, 128], BF16), sbm.alloc([8, 128], BF16), sbm.alloc([8, 256], BF16))
    xs = sbm.alloc([4, 516], BF16)
    cdiag = sbm.alloc([4, 4, 128], BF16)
    acc = sbm.alloc([2, 512], F32)
    xc = sbm.alloc([2, 2, 512], F32)
    xcb = sbm.alloc([4, 512], BF16)
    zs = sbm.alloc([2, 2, 512], BF16)
    X_tok = sbm.alloc([3, 256], BF16)
    Xd_tok = sbm.alloc([3, 256], BF16)
    B_tok = sbm.alloc([3, 128], BF16)
    rhsA_b = sbm.alloc([2, 512], BF16)
    lmat_b = sbm.alloc([2, 512], BF16)
    Ebc_b = sbm.alloc([2, 512], BF16)
    CBm_b = sbm.alloc([2, 128], BF16)
    MT_b = sbm.alloc([2, 512], BF16)
    Cs_b = sbm.alloc([2, 512], BF16)
    Hst = sbm.alloc([256], F32)
    Hb = sbm.alloc([2, 256], BF16)
    t1 = acc
    ab = Bump(A, MIX_COMMON_END)
    watt = [tuple(ab.alloc([8, 128], BF16) for _ in range(3)) for _ in range(2)]
    att_bufs = [dict(qT=ab.alloc([2, S], BF16), kT=ab.alloc([S], BF16), Vaug=ab.alloc([16, 2, 128], BF16))
                for _ in range(2)]
    pT = ab.alloc([4, 512], BF16)
    rz = ab.alloc([512], F32)

    def wview(w2d):
        return w2d.rearrange("(kc p) n -> p kc n", p=128)

    units = []
    state = {"ffn_i": 0, "att_i": 0}

    def plan():
        for s in range(n_seq):
            for l in range(n_layers):
                for gi in range(len(FFN_GROUPS)):
                    units.append(("ffn", s, l, 1, gi))
                units.append(("dt", s, l))
                units.append(("ssd", s, l, 0))
                units.append(("ssd", s, l, 1))
                units.append(("wo", s, l, 0))
                units.append(("ssd", s, l, 2))
                units.append(("ssd", s, l, 3))
                units.append(("wo", s, l, 1))
                for c in range(4):
                    units.append(("att", s, l, c))
                units.append(("wo", s, l, 2))
                for c in range(4, 8):
                    units.append(("att", s, l, c))
                units.append(("wo", s, l, 3))
                for gi in range(len(FFN_GROUPS)):
                    units.append(("ffn", s, l, 2, gi))

    plan()
    loaded = {}
    wpos = {"next_load": 0, "next_get": 0}

    def load_unit(u):
        kind = u[0]
        if kind == "ffn":
            _, s, l, f, gi = u
            J0, J1 = FFN_GROUPS[gi]
            G = J1 - J0
            slot = ffn_slots[state["ffn_i"] % 2]
            state["ffn_i"] += 1
            Wg = slot[0][:, :, 0:G * 128]
            Wu = slot[1][:, :, 0:G * 128]
            Wd = slot[2][:, 0:G, :]
            P.dma(Wg, wview(w_g[f][l])[:, :, J0 * 128:J1 * 128], "pool")
            P.dma(Wu, wview(w_u[f][l])[:, :, J0 * 128:J1 * 128], "pool")
            P.dma(Wd, wview(w_d[f][l])[:, J0:J1, :], "pool")
            return (Wg, Wu, Wd, G)
        if kind == "dt":
            _, s, l = u
            P.dma(wdt, wview(w_in[l])[:, :, DT0:DT0 + 16], "pool")
            return wdt
        if kind == "ssd":
            _, s, l, g = u
            wv = wview(w_in[l])
            P.dma(wssd[0], wv[:, :, X0 + 256 * g: X0 + 256 * g + 256], "pool")
            P.dma(wssd[1], wv[:, :, B0 + 128 * g: B0 + 128 * g + 128], "pool")
            P.dma(wssd[2], wv[:, :, C0 + 128 * g: C0 + 128 * g + 128], "pool")
            P.dma(wssd[3], wv[:, :, Z0 + 256 * g: Z0 + 256 * g + 256], "pool")
            return wssd
        if kind == "att":
            _, s, l, c = u
            wv = wview(w_in[l])
            slot = watt[state["att_i"] % 2]
            state["att_i"] += 1
            P.dma(slot[0], wv[:, :, Q0 + 128 * c: Q0 + 128 * c + 128], "pool")
            P.dma(slot[1], wv[:, :, K0 + 128 * c: K0 + 128 * c + 128], "pool")
            P.dma(slot[2], wv[:, :, V0 + 128 * c: V0 + 128 * c + 128], "pool")
            return slot
        if kind == "wo":
            _, s, l, q4 = u
            P.dma(wo_slot, wview(w_out[l])[:, 4 * q4:4 * q4 + 4, :], "pool")
            return wo_slot
        raise ValueError(kind)

    def wget(u):
        return load_unit(u)

    dumped = set()

    def dump(name, ap):
        if dbg is None or name not in dbg or name in dumped:
            return
        dumped.add(name)
        shp = [int(v) for v in ap.shape]
        d = nc.dram_tensor("dbg_" + name, shp, F32, kind="ExternalOutput").ap()
        P.dma(d, ap, "pool", is_out=True)

    P.dma(cf[:, :], cf_d, "sp")
    P.dma(cb[:, :], cb_d, "pool")
    P.dma(am[:, :, :], am_d.rearrange("p (a b) -> p a b", a=9), "pool")
    P.dma(pv[:, :, :], pvec_d.rearrange("l p n -> p l n"), "sp")

    def pcol(l, c):
        return pv[:, l, c:c + 1]

    def norm_tile(l, base, T):
        tok = slice(T * 512, (T + 1) * 512)
        ss = ps[T % 2]
        for c in range(NCH):
            P.act(sq[:, c % 2, :], x[:, c, tok], AF.Square)
            P.mm(ss[:, :], ones_b, sq[:, c % 2, :], start=(c == 0), stop=(c == NCH - 1))
        P.act(lnv, ss[:, :], AF.Ln, bias=EPS, scale=1.0 / D)
        P.act(rstd[:, T % 2, :], lnv, AF.Exp, scale=-0.5)
        for c in range(NCH):
            P.stt(hT[:, c, tok], x[:, c, tok], pcol(l, base + c), rstd[:, T % 2, :], ALU.mult, ALU.mult)

    def norm_phase(l, base):
        for T in range(4):
            norm_tile(l, base, T)

    def ffn_phase(s, l, f, tail=None):
        for gi in range(len(FFN_GROUPS)):
            Wg, Wu, Wd, G = wget(("ffn", s, l, f, gi))
            last = gi == len(FFN_GROUPS) - 1
            for T in range(4):
                if last and tail is not None and T >= 1:
                    pending_norm = (tail[0], tail[1], T - 1)
                else:
                    pending_norm = None
                tok = slice(T * 512, (T + 1) * 512)
                aT = aTb[T % 2]
                for jj in range(G):
                    gp = ps[jj % 2]
                    up = ps[2 + jj % 2]
                    for kc in range(NCH):
                        P.mm(gp[:, :], Wg[:, kc, jj * 128:(jj + 1) * 128], hT[:, kc, tok], kc == 0, kc == NCH - 1)
                    for kc in range(NCH):
                        P.mm(up[:, :], Wu[:, kc, jj * 128:(jj + 1) * 128], hT[:, kc, tok], kc == 0, kc == NCH - 1)
                    P.act(sgb[jj % 2], gp[:, :], AF.Silu)
                    P.tt(aT[:, jj, :], up[:, :], sgb[jj % 2], ALU.mult)
                for dc in range(NCH):
                    yp = ps[4 + dc % 4]
                    for jj in range(G):
                        P.mm(yp[:, :], Wd[:, jj, dc * 128:(dc + 1) * 128], aT[:, jj, :], jj == 0, jj == G - 1)
                    P.stt(x[:, dc, tok], yp[:, :], 0.5, x[:, dc, tok], ALU.mult, ALU.add)
                if pending_norm is not None:
                    norm_tile(*pending_norm)
            if last and tail is not None:
                norm_tile(tail[0], tail[1], 3)

    def dt_phase(s, l):
        W = wget(("dt", s, l))
        dtp = ps[2]
        for tc in range(16):
            for kc in range(NCH):
                P.mm(dtp[:, tc * 16:(tc + 1) * 16], hT[:, kc, tc * 128:(tc + 1) * 128], W[:, kc, :], kc == 0, kc == NCH - 1)
        v3 = lambda a: a.rearrange("p (a b) -> p a b", a=16)
        bias_b = pv[:, l, PV_DTB:PV_DTB + 16].unsqueeze(1).broadcast_to([128, 16, 16])
        P.tt(v3(dt_tmp), v3(dtp[:, 0:256]), bias_b, ALU.add)
        P.act(dt_tmp, dt_tmp, AF.Exp)
        P.act(dt_tok, dt_tmp, AF.Ln, bias=1.0)
        P.act(a_bc, pv[:, l, PV_ALOG:PV_ALOG + 16], AF.Exp)
        P.stt(v3(adt), v3(dt_tok), -1.0, a_bc.unsqueeze(1).broadcast_to([128, 16, 16]), ALU.mult, ALU.mult)
        P.copy(adt_b, adt, eng="act")
        decp = ps[3]
        cdp = ps[4]
        for tc in range(16):
            P.mm(decp[:, tc * 16:(tc + 1) * 16], su, adt[:, tc * 16:(tc + 1) * 16])
            P.mm(cdp[:, tc * 16:(tc + 1) * 16], ones_f, adt[:, tc * 16:(tc + 1) * 16])
        P.act(decay_tok, decp[:, 0:256], AF.Exp)
        P.act(cd_bc, cdp[:, 0:256], AF.Exp)
        dump("dt_tok", dt_tok); dump("adt", adt); dump("decay_tok", decay_tok); dump("cd_bc", cd_bc)

    tpb = ps[2][:, :].bitcast(BF16)

    ssd_pref = {}

    def ssd_gen(s, l, g):
        W = ssd_pref.pop((s, l, g), None)
        if W is None:
            W = wget(("ssd", s, l, g))
        P.memset(Hst, 0.0)
        P.memset(Hb[:, 0, :], 0.0)
        P.memset(xs[:, :, 0:3], 0.0)
        hcur = 0
        cidx = [2 * g, 2 * g + 1, 8 + g, 12 + g]
        for ch in range(4):
            for k in range(4):
                P.ts(cdiag[:, ch, k, :], ident, pcol(l, PV_CW + k * 16 + cidx[ch]), None, ALU.mult)
        h4 = lambda a, n: a.rearrange("p (h n) -> p h n", h=4)
        def Pm(T, ch):
            tk = slice(T * 512, (T + 1) * 512)
            pp = ps[ch % 2]
            Wsub, col = ((W[0], 0), (W[0], 128), (W[1], 0), (W[2], 0), (W[3], 0), (W[3], 128))[ch]
            for kc in range(NCH):
                P.mm(pp[:, :], Wsub[:, kc, col:col + 128], hT[:, kc, tk], kc == 0, kc == NCH - 1)

        def Cv(T, ch):
            pp = ps[ch % 2]
            P.copy(xs[:, ch, 3:515], pp[:, :], eng="act")
            ci = cidx[ch]
            cps = ps[3 + ch % 2]
            for k in range(4):
                P.mm(cps[:, :], cdiag[:, ch, k, :], xs[:, ch, k:k + 512], k == 0, k == 3)
            P.copy(xs[:, ch, 0:3], xs[:, ch, 512:515])
            if ch < 2:
                P.act(xc[:, T % 2, ch, :], cps[:, :], AF.Silu, bias=pcol(l, PV_CB + ci))
            P.act(xcb[:, ch, :], cps[:, :], AF.Silu, bias=pcol(l, PV_CB + ci))

        def epi(T):
            tk = slice(T * 512, (T + 1) * 512)
            yps_ = (ps[6], ps[7])
            for cc in range(2):
                chunk = 2 * g + cc
                P.stt(t1[:, cc, :], xc[:, T % 2, cc, :], pcol(l, PV_DSKIP + chunk), yps_[cc][:, :], ALU.mult, ALU.add)
                P.tt(t1[:, cc, :], t1[:, cc, :], zs[:, T % 2, cc, :], ALU.mult)
                P.act(sq[:, cc, :], t1[:, cc, :], AF.Square)
                P.mm(ps[3][:, :], ones_b, sq[:, cc, :], cc == 0, cc == 1)
            P.act(lnv, ps[3][:, :], AF.Ln, bias=EPS, scale=1.0 / 256)
            P.act(rstd[:, 0, :], lnv, AF.Exp, scale=-0.5)
            for cc in range(2):
                chunk = 2 * g + cc
                P.stt(yT[:, chunk % 4, tk], t1[:, cc, :], pcol(l, PV_SSDN + chunk), rstd[:, 0, :], ALU.mult, ALU.mult)

        Pm(0, 0)
        Pm(0, 1)
        for T in range(4):
            tok = slice(T * 512, (T + 1) * 512)
            for fn_, a_ in ((Cv, 0), (Pm, 2), (Cv, 1), (Pm, 3), (Cv, 2), (Pm, 4), (Cv, 3), (Pm, 5)):
                fn_(T, a_)
                yield
            if T == 3 and g + 1 < 4:
                ssd_pref[(s, l, g + 1)] = wget(("ssd", s, l, g + 1))
            P.act(zs[:, T % 2, 0, :], ps[0][:, :], AF.Silu)
            P.act(zs[:, T % 2, 1, :], ps[1][:, :], AF.Silu)
            yps = (ps[6], ps[7])
            if T >= 1:
                epi(T - 1)
            hbox = [hcur]

            def stA(tc4):
                c0 = tc4 * 128
                tcg = 4 * T + tc4
                tpo = (tc4 % 2) * 512
                for ch in range(3):
                    P.tr(tpb[:, tpo + ch * 128: tpo + (ch + 1) * 128], xcb[:, ch, c0:c0 + 128], ident)
                hsl = slice(tcg * 16 + 4 * g, tcg * 16 + 4 * g + 4)
                Xt = X_tok[:, tc4 % 3, :]
                Xd = Xd_tok[:, tc4 % 3, :]
                P.tt(h4(Xt, 64), h4(tpb[:, tpo:tpo + 256], 64),
                     dt_tok[:, hsl].unsqueeze(2).broadcast_to([128, 4, 64]), ALU.mult)
                P.copy(B_tok[:, tc4 % 3, :], tpb[:, tpo + 256:tpo + 384], eng="act")
                P.tt(h4(rhsA_b[:, tc4 % 2, :], 128), tri_b.unsqueeze(1).broadcast_to([128, 4, 128]),
                     adt_b[:, hsl].unsqueeze(2).broadcast_to([128, 4, 128]), ALU.mult)
                P.tt(h4(Xd, 64), h4(Xt, 64), decay_tok[:, hsl].unsqueeze(2).broadcast_to([128, 4, 64]), ALU.mult)

            def stB(tc4):
                c0 = tc4 * 128
                rhsA = rhsA_b[:, tc4 % 2, :]
                lmat = lmat_b[:, tc4 % 2, :]
                Ebc = Ebc_b[:, tc4 % 2, :]
                CBm = CBm_b[:, tc4 % 2, :]
                P.mm(ps[3][:, :], su_b, rhsA)
                P.mm(ps[4][:, :], ones_b, rhsA)
                P.mm(ps[5][:, 0:128], xcb[:, 2, c0:c0 + 128], xcb[:, 3, c0:c0 + 128])
                P.act(lmat, ps[3][:, :], AF.Exp)
                P.act(Ebc, ps[4][:, :], AF.Exp)
                P.tt(CBm, ps[5][:, 0:128], tri, ALU.mult)
                P.tt(h4(MT_b[:, tc4 % 2, :], 128), h4(lmat, 128), CBm.unsqueeze(1).broadcast_to([128, 4, 128]), ALU.mult)
                P.tt(h4(Cs_b[:, tc4 % 2, :], 128), h4(Ebc, 128),
                     xcb[:, 3, c0:c0 + 128].unsqueeze(1).broadcast_to([128, 4, 128]), ALU.mult)

            def stC(tc4):
                c0 = tc4 * 128
                tcg = 4 * T + tc4
                hsl = slice(tcg * 16 + 4 * g, tcg * 16 + 4 * g + 4)
                Xt = X_tok[:, tc4 % 3, :]
                Xd = Xd_tok[:, tc4 % 3, :]
                MT = MT_b[:, tc4 % 2, :]
                Cs = Cs_b[:, tc4 % 2, :]
                hc = hbox[0]
                P.mm(ps[5][:, 128:384], B_tok[:, tc4 % 3, :], Xd)
                for h in range(4):
                    cdcol = cd_bc[:, tcg * 16 + 4 * g + h: tcg * 16 + 4 * g + h + 1]
                    P.stt(Hst[:, h * 64:(h + 1) * 64], Hst[:, h * 64:(h + 1) * 64], cdcol,
                          ps[5][:, 128 + h * 64:128 + (h + 1) * 64], ALU.mult, ALU.add)
                P.copy(Hb[:, 1 - hc, :], Hst, eng="act")
                for h in range(4):
                    cc, hh = h // 2, h % 2
                    yo = yps[cc][hh * 64:(hh + 1) * 64, c0:c0 + 128]
                    P.mm(yo, Xt[:, h * 64:(h + 1) * 64], MT[:, h * 128:(h + 1) * 128], True, False)
                    P.mm(yo, Hb[:, hc, h * 64:(h + 1) * 64], Cs[:, h * 128:(h + 1) * 128], False, True)
                hbox[0] = 1 - hc

            nxt = []
            if T < 3:
                tkn = slice((T + 1) * 512, (T + 2) * 512)
                for ch_ in range(2):
                    Wsub_, col_ = ((W[0], 0), (W[0], 128))[ch_]
                    for kc in range(NCH):
                        nxt.append((ps[ch_], Wsub_[:, kc, col_:col_ + 128], hT[:, kc, tkn], kc == 0, kc == NCH - 1))
            for si, (st, i) in enumerate((("A", 0), ("A", 1), ("B", 0), ("A", 2), ("B", 1), ("C", 0), ("A", 3), ("B", 2),
                                          ("C", 1), ("B", 3), ("C", 2), ("C", 3))):
                {"A": stA, "B": stB, "C": stC}[st](i)
                if si >= 4:
                    for _ in range(2):
                        if nxt:
                            o_, l_, r_, st_, sp_ = nxt.pop(0)
                            P.mm(o_[:, :], l_, r_, st_, sp_)
                yield
            for o_, l_, r_, st_, sp_ in nxt:
                P.mm(o_[:, :], l_, r_, st_, sp_)
            hcur = hbox[0]
        epi(3)

    def dump_yT(tag):
        dump("yT_" + tag, yT[:, :, :])

    def att_proj_gen(s, l, c, B):
        qT, kT, Vaug = B["qT"], B["kT"], B["Vaug"]
        W = wget(("att", s, l, c))
        psv = ps[0][:, :].rearrange("p (a h e) -> p a h e", a=4, h=2)
        for T in range(4):
            tok = slice(T * 512, (T + 1) * 512)
            for kc in range(NCH):
                P.mm(ps[0][:, :], W[0][:, kc, :], hT[:, kc, tok], kc == 0, kc == NCH - 1)
            yield
            P.act(sq[:, 0, :], ps[0][:, :], AF.Square)
            for kc in range(NCH):
                P.mm(ps[1][:, :], W[1][:, kc, :], hT[:, kc, tok], kc == 0, kc == NCH - 1)
            yield
            P.act(sq[:, 1, :], ps[1][:, :], AF.Square)
            P.mm(ps[2][:, :], blk_b, sq[:, 0, :])
            yield
            P.act(lnv, ps[2][:, :], AF.Ln, bias=EPS, scale=1.0 / 64)
            P.act(rstd[:, 0, :], lnv, AF.Exp, scale=-0.5)
            yield
            P.mm(ps[2][:, :], blk_b, sq[:, 1, :])
            P.stt(qT[0:64, 0, tok], ps[0][0:64, :], pv[0:64, l, PV_QN:PV_QN + 1], rstd[0:64, 0, :], ALU.mult, ALU.mult)
            P.stt(qT[64:128, 1, tok], ps[0][64:128, :], pv[64:128, l, PV_QN:PV_QN + 1], rstd[64:128, 0, :],
                  ALU.mult, ALU.mult)
            yield
            P.act(lnv, ps[2][:, :], AF.Ln, bias=EPS, scale=1.0 / 64)
            P.act(rstd[:, 1, :], lnv, AF.Exp, scale=-0.5)
            yield
            P.stt(kT[:, tok], ps[1][:, :], pcol(l, PV_KN), rstd[:, 1, :], ALU.mult, ALU.mult)
            for kb4 in range(4):
                kb = 4 * T + kb4
                for kc in range(NCH):
                    P.mm(ps[0][:, kb4 * 128:(kb4 + 1) * 128], hT[:, kc, kb * 128:(kb + 1) * 128], W[2][:, kc, :],
                         kc == 0, kc == NCH - 1)
            yield
            P.copy(Vaug[:, 4 * T:4 * T + 4, 0, 0:64], psv[:, :, 0, :])
            P.copy(Vaug[:, 4 * T:4 * T + 4, 1, 64:128], psv[:, :, 1, :])
            yield

    def att_blocks_gen(s, l, c, B):
        qT, kT, Vaug = B["qT"], B["kT"], B["Vaug"]
        blocks = []
        for hh in range(2):
            for T in range(4):
                nj = 4 * T + 4
                for j in range(nj):
                    blocks.append((hh, T, j, nj))
        sbank = (ps[3], ps[4], ps[5])

        def cols(T, j):
            r = j - 4 * T
            return 128 * r if r > 0 else 0

        def qk(i):
            hh, T, j, nj = blocks[i]
            Up = slice(hh * 64, hh * 64 + 64)
            q0 = cols(T, j)
            P.mm(sbank[i % 3][:, q0:512], kT[:, j * 128:(j + 1) * 128], qT[:, hh, T * 512 + q0:(T + 1) * 512])

        LOOK = 2
        pending = []
        for i in range(min(LOOK, len(blocks))):
            qk(i)
        for i, (hh, T, j, nj) in enumerate(blocks):
            if i + LOOK < len(blocks):
                qk(i + LOOK)
            Up = slice(hh * 64, hh * 64 + 64)
            Zp = slice((1 - hh) * 64, (1 - hh) * 64 + 64)
            tok = slice(T * 512, (T + 1) * 512)
            q0 = cols(T, j)
            op_ = ps[6 + (T % 2)]
            pt = pT[:, i % 4, :]
            P.act(pt[:, q0:512], sbank[i % 3][:, q0:512], AF.Exp, scale=0.125)
            m = 4 * T - j
            mi = m + 3 if m <= 4 else 8
            P.tt(pt[:, q0:512], pt[:, q0:512], am[:, mi, q0:512], ALU.mult)
            P.mm(op_[:, q0:512], Vaug[:, j, hh, :], pt[:, q0:512], j == 0, j == nj - 1)
            if j == nj - 1:
                pending.append((i + 3, Up, Zp, tok, op_))
            while pending and pending[0][0] <= i:
                _, Up_, Zp_, tok_, o_ = pending.pop(0)
                P.act(rz[Up_, :], o_[Zp_, :], AF.Ln)
                P.act(rz[Up_, :], rz[Up_, :], AF.Exp, scale=-1.0)
                P.tt(yT[Up_, c % 4, tok_], o_[Up_, :], rz[Up_, :], ALU.mult)
            yield
        for _, Up_, Zp_, tok_, o_ in pending:
            P.act(rz[Up_, :], o_[Zp_, :], AF.Ln)
            P.act(rz[Up_, :], rz[Up_, :], AF.Exp, scale=-1.0)
            P.tt(yT[Up_, c % 4, tok_], o_[Up_, :], rz[Up_, :], ALU.mult)

    def interleave(main, sides):
        sides = [[g, r, 0.0, False] for (g, r) in sides if g is not None]
        for _ in main:
            for sd in sides:
                if sd[3]:
                    continue
                sd[2] += sd[1]
                while sd[2] >= 1.0 and not sd[3]:
                    sd[2] -= 1.0
                    try:
                        next(sd[0])
                    except StopIteration:
                        sd[3] = True
        for sd in sides:
            if not sd[3]:
                for _ in sd[0]:
                    pass

    def att_all(s, l, tail=None):
        for B in att_bufs:
            P.memset(B["Vaug"][:, :, 0, 64:128], 1.0)
            P.memset(B["Vaug"][:, :, 1, 0:64], 1.0)
            P.memset(B["qT"][64:128, 0, :], 0.0)
            P.memset(B["qT"][0:64, 1, :], 0.0)
        run(att_proj_gen(s, l, 0, att_bufs[0]))
        for c in range(8):
            sides = []
            if c + 1 < 8:
                sides.append((att_proj_gen(s, l, c + 1, att_bufs[(c + 1) % 2]), 0.5))
            interleave(att_blocks_gen(s, l, c, att_bufs[c % 2]), sides)
            if c == 3:
                out_proj(s, l, 2)
        out_proj(s, l, 3, tail)

    def out_proj_gen(s, l, q4, tail=None, banks=(0, 1)):
        Wo = wget(("wo", s, l, q4))
        for T in range(4):
            tok = slice(T * 512, (T + 1) * 512)
            for dc in range(NCH):
                yp = ps[banks[dc % 2]]
                for kc in range(4):
                    P.mm(yp[:, :], Wo[:, kc, dc * 128:(dc + 1) * 128], yT[:, kc, tok], kc == 0, kc == 3)
                P.tt(x[:, dc, tok], yp[:, :], x[:, dc, tok], ALU.add)
                yield
            if tail is not None and T >= 1:
                norm_tile(tail[0], tail[1], T - 1)
        if tail is not None:
            norm_tile(tail[0], tail[1], 3)

    def run(gen):
        for _ in gen:
            pass

    def out_proj(s, l, q4, tail=None):
        run(out_proj_gen(s, l, q4, tail))

    nphase = [0]

    def go():
        nphase[0] += 1
        return stop_after is None or nphase[0] <= stop_after

    fuse = (stop_after is None) or FORCE_FUSE

    def layers(s):
        for l in range(n_layers):
            if not go():
                return
            if l == 0 or not fuse:
                norm_phase(l, PV_FFN1)
            ffn_phase(s, l, 1, tail=(l, PV_MIX) if fuse else None)
            if not go():
                return
            if not fuse:
                norm_phase(l, PV_MIX)
            dt_phase(s, l)
            run(ssd_gen(s, l, 0))
            run(ssd_gen(s, l, 1))
            dump_yT("ssd01")
            out_proj(s, l, 0)
            run(ssd_gen(s, l, 2))
            run(ssd_gen(s, l, 3))
            out_proj(s, l, 1)
            if not go():
                return
            if fuse:
                att_all(s, l, tail=(l, PV_FFN2))
            else:
                att_all_plain(s, l)
            if not go():
                return
            if not fuse:
                norm_phase(l, PV_FFN2)
            ffn_phase(s, l, 2, tail=(l + 1, PV_FFN1) if (fuse and l + 1 < n_layers) else None)

    def att_all_plain(s, l):
        for B in att_bufs:
            P.memset(B["Vaug"][:, :, 0, 64:128], 1.0)
            P.memset(B["Vaug"][:, :, 1, 0:64], 1.0)
            P.memset(B["qT"][64:128, 0, :], 0.0)
            P.memset(B["qT"][0:64, 1, :], 0.0)
        run(att_proj_gen(s, l, 0, att_bufs[0]))
        for c in range(8):
            sides = []
            if c + 1 < 8:
                sides.append((att_proj_gen(s, l, c + 1, att_bufs[(c + 1) % 2]), 0.5))
            interleave(att_blocks_gen(s, l, c, att_bufs[c % 2]), sides)
            if c == 3:
                out_proj(s, l, 2)
        out_proj(s, l, 3)

    for s in range(n_seq):
        for c in range(NCH):
            P.dma(x[:, c, :], xT[s, c * 128:(c + 1) * 128, :], "sp")
        layers(s)
        for c in range(NCH):
            P.dma(outT[s, c * 128:(c + 1) * 128, :], x[:, c, :], "sp", is_out=True)
        if stop_after is not None:
            break
    P.emit(es)
    es.close()
    return nc


def make_consts():
    k = np.arange(128)
    tri = (k[:, None] <= k[None, :]).astype(np.float32)
    su = (k[:, None] > k[None, :]).astype(np.float32)
    ones = np.ones((128, 128), np.float32)
    cf = np.concatenate([tri, su, ones], axis=1)
    ident = np.eye(128, dtype=np.float32)
    blk = np.zeros((128, 128), np.float32)
    blk[:64, :64] = 1.0
    blk[64:, 64:] = 1.0
    cb = np.concatenate([ident, ones, blk, su, tri], axis=1)
    am = np.zeros((128, 9, 512), np.float32)
    kk = np.arange(128)[:, None]
    qq = np.arange(512)[None, :]
    for mi in range(8):
        m = mi - 3
        d = 128 * m + qq - kk
        c = ((d >= 0) & (d <= 128)).astype(np.float32)
        c += ((d >= 0) & (d % 4 == 0) & (d <= 512)).astype(np.float32)
        c += ((d >= 0) & (d % 16 == 0)).astype(np.float32)
        am[:, mi, :] = c
    am[:, 8, :] = (((qq - kk) % 16) == 0).astype(np.float32)
    return cf, cb, np.ascontiguousarray(am.reshape(128, 9 * 512))


def make_pvec(inp):
    pv = np.zeros((2, 128, NV), np.float32)
    p = np.arange(128)
    for l in range(2):
        def cols(v, n):
            return np.asarray(v, np.float32).reshape(n, 128).T
        pv[l, :, PV_FFN1:PV_FFN1 + 8] = cols(inp["ffn1_norm"][l], 8)
        pv[l, :, PV_MIX:PV_MIX + 8] = cols(inp["mix_norm"][l], 8)
        pv[l, :, PV_FFN2:PV_FFN2 + 8] = cols(inp["ffn2_norm"][l], 8)
        pv[l, :, PV_SSDN:PV_SSDN + 8] = cols(inp["ssd_norm"][l], 8)
        ds = np.asarray(inp["d_skip"][l], np.float32)
        for c in range(8):
            pv[l, :, PV_DSKIP + c] = ds[2 * c + p // 64]
        pv[l, :, PV_QN] = np.asarray(inp["q_norm"][l], np.float32)[p % 64]
        pv[l, :, PV_KN] = np.asarray(inp["k_norm"][l], np.float32)[p % 64]
        pv[l, :, PV_CB:PV_CB + 16] = cols(inp["conv_b"][l], 16)
        cw = np.asarray(inp["conv_w"][l], np.float32)
        for k in range(4):
            pv[l, :, PV_CW + k * 16:PV_CW + (k + 1) * 16] = cols(cw[k], 16)
        pv[l, :, PV_DTB:PV_DTB + 16] = np.asarray(inp["dt_bias"][l], np.float32)[None, :]
        pv[l, :, PV_ALOG:PV_ALOG + 16] = np.asarray(inp["a_log"][l], np.float32)[None, :]
    return pv


_W_NAMES = ["ffn1_w_gate", "ffn1_w_up", "ffn1_w_down", "ffn2_w_gate", "ffn2_w_up", "ffn2_w_down", "w_in", "w_out"]


def kernel(**inputs):
    x = np.asarray(inputs["x"], np.float32)
    nb = x.shape[0]
    assert nb == N_CORES * SEQ_PER_CORE
    cf, cb, am = make_consts()
    pvec = make_pvec(inputs)
    shared = {n: np.ascontiguousarray(np.asarray(inputs[n], np.float32)) for n in _W_NAMES}
    shared.update({"pvec": pvec, "cf": cf, "cb": cb, "am": am})
    in_maps = []
    for core in range(N_CORES):
        xs = x[core * SEQ_PER_CORE:(core + 1) * SEQ_PER_CORE]
        m = dict(shared)
        m["xT"] = np.ascontiguousarray(xs.transpose(0, 2, 1))
        in_maps.append(m)
    nc = build_program()
    res = run_bass_kernel_spmd(nc, in_maps, core_ids=list(range(N_CORES)))
    out = np.empty((nb, S, D), np.float32)
    for core in range(N_CORES):
        o = np.asarray(res.results[core]["outT"])
        out[core * SEQ_PER_CORE:(core + 1) * SEQ_PER_CORE] = o.transpose(0, 2, 1)
    return out
```

```python
import numpy as np
from contextlib import ExitStack
import concourse.bass as bass
import concourse.mybir as mybir
from concourse.bass_utils import run_bass_kernel_spmd

F32 = mybir.dt.float32
BF16 = mybir.dt.bfloat16
AF = mybir.ActivationFunctionType
ALU = mybir.AluOpType

S = 2048
D = 1024
DFF = 2816
NCH = 8
EPS = 1e-6
N_CORES = 8
SEQ_PER_CORE = 2

Z0, X0, B0, C0, DT0, Q0, K0, V0 = 0, 1024, 2048, 2560, 3072, 3088, 4112, 5136

PV_FFN1, PV_MIX, PV_FFN2, PV_SSDN, PV_DSKIP, PV_QN, PV_KN, PV_CB, PV_CW, PV_DTB, PV_ALOG, NV = (
    0, 8, 16, 24, 32, 40, 41, 42, 58, 122, 138, 154)

ARENA_BYTES = 97 * 1024
FFN_GROUPS = [(0, 6), (6, 12), (12, 17), (17, 22)]
N_DMA_SEM = {"pool": 24, "sp": 20}
FORCE_FUSE = False
EMBED_WAIT = True


def _esz(dt):
    return 4 if dt == F32 else 2


class _Op:
    __slots__ = ("eng", "fn", "waits", "sig", "dma", "vc")

    def __init__(self, eng, fn):
        self.eng = eng
        self.fn = fn
        self.waits = []
        self.sig = False
        self.dma = None


class Prog:
    ENGS = ("pe", "act", "dve", "pool", "sp")

    def __init__(self, nc):
        self.nc = nc
        self.ops = {e: [] for e in self.ENGS}
        self.recs = {}
        self.tinfo = {}
        self.waited = {e: {} for e in self.ENGS}
        self.dma_rr = {"pool": 0, "sp": 0}
        self.dma_cnt = {}
        self.out_tokens = []
        self.evc = {e: {} for e in self.ENGS}
        self.dma_vc = {}

    def register(self, name, pstride_bytes):
        self.tinfo[name] = pstride_bytes
        self.recs[name] = []

    def region(self, ap):
        name = ap.tensor.name
        pb = self.tinfo.get(name)
        if pb is None:
            return None
        esz = _esz(ap.dtype)
        pstride = pb // esz
        off = ap.offset
        dims = ap.ap
        p0 = off // pstride
        fo = off - p0 * pstride
        ext = 0
        for st, cn in dims[1:]:
            ext += (cn - 1) * abs(st)
        if name.startswith("ps"):
            return (name, 0, 128, 0, 2048)
        return (name, p0, p0 + dims[0][1], fo * esz, (fo + ext + 1) * esz)

    def add(self, eng, fn, reads, writes, dma=False, is_out=False):
        op = _Op(eng, fn)
        deps = {}
        rregs = [r for r in (self.region(a) for a in reads) if r is not None]
        wregs = [r for r in (self.region(a) for a in writes) if r is not None]
        for (name, p0, p1, b0, b1) in rregs:
            isps = name.startswith("ps")
            for r in self.recs[name]:
                if (r[2] or (isps and r[0] != eng)) and r[3] < p1 and p0 < r[4] and r[5] < b1 and b0 < r[6]:
                    if r[0] == "pe" and eng == "pe":
                        continue
                    if deps.get(r[0], -1) < r[1]:
                        deps[r[0]] = r[1]
        for (name, p0, p1, b0, b1) in wregs:
            for r in self.recs[name]:
                if r[3] < p1 and p0 < r[4] and r[5] < b1 and b0 < r[6]:
                    if r[0] == eng and eng == "pe":
                        continue
                    if deps.get(r[0], -1) < r[1]:
                        deps[r[0]] = r[1]
        if dma:
            k = self.dma_rr[eng]
            self.dma_rr[eng] = (k + 1) % N_DMA_SEM[eng]
            key = ("dma", eng, k)
            m = self.dma_cnt.get(key, 0) + 1
            self.dma_cnt[key] = m
            if m > 1:
                if deps.get(key, -1) < 16 * (m - 1):
                    deps[key] = 16 * (m - 1)
            op.dma = (key, 16 * m)
            mykey, myval = key, 16 * m
            if is_out:
                self.out_tokens.append((key, 16 * m))
        else:
            mykey, myval = eng, len(self.ops[eng])
        w = self.evc[eng]
        for key, val in sorted(deps.items(), key=lambda kv: str(kv[0])):
            if w.get(key, -1) >= val:
                continue
            op.waits.append((key, val))
            if isinstance(key, tuple):
                src = self.dma_vc.get((key, val), {})
            else:
                self.ops[key][val].sig = True
                src = self.ops[key][val].vc
            for k2, v2 in src.items():
                if w.get(k2, -1) < v2:
                    w[k2] = v2
            if w.get(key, -1) < val:
                w[key] = val
        if dma:
            self.dma_vc[(mykey, myval)] = dict(w)
            op.vc = None
        else:
            op.vc = dict(w)
            op.vc[eng] = max(op.vc.get(eng, -1), len(self.ops[eng]))
        self.ops[eng].append(op)
        for (name, p0, p1, b0, b1) in wregs:
            lst = self.recs[name]
            lst[:] = [r for r in lst if not (p0 <= r[3] and r[4] <= p1 and b0 <= r[5] and r[6] <= b1)]
            lst.append((mykey, myval, True, p0, p1, b0, b1))
        for (name, p0, p1, b0, b1) in rregs:
            lst = self.recs[name]
            lst[:] = [r for r in lst if not ((not r[2]) and r[0] == mykey and p0 <= r[3] and r[4] <= p1
                                             and b0 <= r[5] and r[6] <= b1)]
            lst.append((mykey, myval, False, p0, p1, b0, b1))
        return op

    def mm(self, out, lhsT, rhs, start=True, stop=True):
        self.add("pe", lambda e: e.matmul(out, lhsT=lhsT, rhs=rhs, start=start, stop=stop),
                 [lhsT, rhs], [out])

    def tr(self, out, in_, ident):
        self.add("pe", lambda e: e.transpose(out, in_, ident), [in_, ident], [out])

    def act(self, out, in_, func, bias=None, scale=None):
        kw = {}
        reads = [in_]
        if bias is not None:
            kw["bias"] = bias
            if not isinstance(bias, float):
                reads.append(bias)
        if scale is not None:
            kw["scale"] = scale
        self.add("act", lambda e: e.activation(out=out, in_=in_, func=func, **kw), reads, [out])

    def tt(self, out, in0, in1, op, eng="dve"):
        self.add(eng, lambda e: e.tensor_tensor(out=out, in0=in0, in1=in1, op=op), [in0, in1], [out])

    def ts(self, out, in0, s1, s2, op0, op1=None, eng="dve"):
        reads = [in0] + [s for s in (s1, s2) if s is not None and not isinstance(s, float)]
        if op1 is None:
            self.add(eng, lambda e: e.tensor_scalar(out=out, in0=in0, scalar1=s1, scalar2=None, op0=op0),
                     reads, [out])
        else:
            self.add(eng, lambda e: e.tensor_scalar(out=out, in0=in0, scalar1=s1, scalar2=s2, op0=op0, op1=op1),
                     reads, [out])

    def stt(self, out, in0, scalar, in1, op0, op1, eng="dve"):
        reads = [in0, in1] + ([] if isinstance(scalar, float) else [scalar])
        self.add(eng, lambda e: e.scalar_tensor_tensor(out=out, in0=in0, scalar=scalar, in1=in1, op0=op0, op1=op1),
                 reads, [out])

    def copy(self, out, in_, eng="dve"):
        if eng == "act":
            self.add("act", lambda e: e.activation(out=out, in_=in_, func=AF.Copy), [in_], [out])
        else:
            self.add(eng, lambda e: e.tensor_copy(out=out, in_=in_), [in_], [out])

    def recip(self, out, in_):
        self.add("dve", lambda e: e.reciprocal(out=out, in_=in_), [in_], [out])

    def memset(self, ap, val, eng="dve"):
        self.add(eng, lambda e: e.memset(ap, val), [], [ap])

    def dma(self, out, in_, q, is_out=False):
        self.add(q, lambda e: e.dma_start(out=out, in_=in_), [in_], [out], dma=True, is_out=is_out)

    def emit(self, es):
        nc = self.nc
        sems = {}
        for e in ("pe", "act", "dve", "pool"):
            sems[e] = es.enter_context(nc.semaphore("s_" + e))
        for q in ("pool", "sp"):
            for k in range(N_DMA_SEM[q]):
                sems[("dma", q, k)] = es.enter_context(nc.semaphore("d_%s_%d" % (q, k)))
        counts = {}
        for e in self.ENGS:
            c = 0
            lst = []
            for op in self.ops[e]:
                if op.sig:
                    c += 1
                lst.append(c)
            counts[e] = lst
        final_waits = list(self.out_tokens)

        def run(ename, e):
            for op in self.ops[ename]:
                ws = [(sems[key], val if isinstance(key, tuple) else counts[key][val]) for key, val in op.waits]
                emb = None
                if EMBED_WAIT and ws and op.dma is None:
                    emb = ws.pop()
                for sm, vv in ws:
                    e.wait_ge(sm, vv)
                ins = op.fn(e)
                if emb is not None:
                    ins._wait_ge(emb[0], emb[1])
                if op.dma is not None:
                    ins.then_inc(sems[op.dma[0]], 16)
                elif op.sig:
                    ins.then_inc(sems[ename], 1)
            if ename == "sp":
                for key, val in final_waits:
                    e.wait_ge(sems[key], val)

        block = es.enter_context(nc.Block())

        @block.tensor
        def _(e):
            run("pe", e)

        @block.scalar
        def _(e):
            run("act", e)

        @block.vector
        def _(e):
            run("dve", e)

        @block.gpsimd
        def _(e):
            run("pool", e)

        @block.sync
        def _(e):
            run("sp", e)


class Arena:
    def __init__(self, base_ap):
        self.base = base_ap

    def at(self, off, shape, dt):
        n = int(np.prod(shape)) * _esz(dt)
        assert off % 4 == 0 and off + n <= ARENA_BYTES, (off, n, ARENA_BYTES)
        n4 = (n + 3) // 4
        v = self.base[:, off // 4: off // 4 + n4]
        if dt == BF16:
            v = v.bitcast(BF16)
            v = v[:, 0:int(np.prod(shape))]
        if len(shape) == 1:
            return v
        names = " ".join("d%d" % i for i in range(len(shape)))
        kw = {"d%d" % i: int(shape[i]) for i in range(len(shape))}
        return v.rearrange("p (%s) -> p %s" % (names, names), **kw)


class Bump:
    def __init__(self, arena, start):
        self.arena = arena
        self.off = start

    def alloc(self, shape, dt):
        n = int(np.prod(shape)) * _esz(dt)
        o = self.off
        self.off = o + ((n + 31) // 32) * 32
        return self.arena.at(o, shape, dt)


def build_program(n_seq=SEQ_PER_CORE, n_layers=2, stop_after=None, dbg=None):
    nc = bass.Bass("TRN2", target_bir_lowering=False)
    P = Prog(nc)
    es = ExitStack()

    def dram(name, shape, kind="ExternalInput"):
        return nc.dram_tensor(name, list(shape), F32, kind=kind).ap()

    xT = dram("xT", [n_seq, D, S])
    outT = dram("outT", [n_seq, D, S], kind="ExternalOutput")
    w_g = {1: dram("ffn1_w_gate", [2, D, DFF]), 2: dram("ffn2_w_gate", [2, D, DFF])}
    w_u = {1: dram("ffn1_w_up", [2, D, DFF]), 2: dram("ffn2_w_up", [2, D, DFF])}
    w_d = {1: dram("ffn1_w_down", [2, DFF, D]), 2: dram("ffn2_w_down", [2, DFF, D])}
    w_in = dram("w_in", [2, D, 6160])
    w_out = dram("w_out", [2, 2048, D])
    pvec_d = dram("pvec", [2, 128, NV])
    cf_d = dram("cf", [128, 384])
    cb_d = dram("cb", [128, 640])
    am_d = dram("am", [128, 9 * 512])

    def sb(name, shape, dt):
        t = es.enter_context(nc.sbuf_tensor("sb_" + name, list(shape), dt))
        P.register("sb_" + name, int(np.prod(shape[1:])) * _esz(dt))
        return t

    x = sb("x", [128, NCH, S], F32)
    hT = sb("hT", [128, NCH, S], BF16)
    am = sb("am", [128, 9, 512], BF16)
    cf = sb("cf", [128, 384], F32)
    cb = sb("cb", [128, 640], BF16)
    pv = sb("pv", [128, 2, NV], F32)
    arena_t = sb("arena", [128, ARENA_BYTES // 4], F32)
    ps = []
    for i in range(8):
        t = es.enter_context(nc.psum_tensor("ps%d" % i, [128, 512], F32))
        P.register("ps%d" % i, 2048)
        ps.append(t)

    tri = cf[:, 0:128]
    su = cf[:, 128:256]
    ones_f = cf[:, 256:384]
    ident = cb[:, 0:128]
    ones_b = cb[:, 128:256]
    blk_b = cb[:, 256:384]
    su_b = cb[:, 384:512]
    tri_b = cb[:, 512:640]

    A = Arena(arena_t[:, :])
    nb = Bump(A, 0)
    sq = nb.alloc([2, 512], BF16)
    lnv = nb.alloc([512], F32)
    rstd = nb.alloc([2, 512], F32)
    COMMON_END = nb.off

    fb = Bump(A, COMMON_END)
    ffn_slots = []
    for i in range(2):
        ffn_slots.append((fb.alloc([8, 768], BF16), fb.alloc([8, 768], BF16), fb.alloc([6, 1024], BF16)))
    aTb = [fb.alloc([6, 512], BF16) for _ in range(2)]
    sgb = [fb.alloc([512], F32) for _ in range(2)]

    mb = Bump(A, COMMON_END)
    yT = mb.alloc([4, S], BF16)
    wo_slot = mb.alloc([4, 1024], BF16)
    wdt = mb.alloc([8, 16], BF16)
    dt_tmp = mb.alloc([256], F32)
    dt_tok = mb.alloc([256], F32)
    adt = mb.alloc([256], F32)
    decay_tok = mb.alloc([256], BF16)
    adt_b = mb.alloc([256], BF16)
    cd_bc = mb.alloc([256], F32)
    a_bc = mb.alloc([16], F32)
    MIX_COMMON_END = mb.off
    sbm = Bump(A, MIX_COMMON_END)
    wssd = (sbm.alloc([8, 256], BF16), sbm.alloc([8, 128], BF16), sbm.alloc([8, 128], BF16), sbm.alloc([8, 256], BF16))
    xs = sbm.alloc([4, 516], BF16)
    cdiag = sbm.alloc([4, 4, 128], BF16)
    acc = sbm.alloc([2, 512], F32)
    xc = sbm.alloc([2, 2, 512], F32)
    xcb = sbm.alloc([4, 512], BF16)
    zs = sbm.alloc([2, 2, 512], BF16)
    X_tok = sbm.alloc([3, 256], BF16)
    Xd_tok = sbm.alloc([3, 256], BF16)
    B_tok = sbm.alloc([3, 128], BF16)
    rhsA_b = sbm.alloc([2, 512], BF16)
    lmat_b = sbm.alloc([2, 512], BF16)
    Ebc_b = sbm.alloc([2, 512], BF16)
    CBm_b = sbm.alloc([2, 128], BF16)
    MT_b = sbm.alloc([2, 512], BF16)
    Cs_b = sbm.alloc([2, 512], BF16)
    Hst = sbm.alloc([256], F32)
    Hb = sbm.alloc([2, 256], BF16)
    t1 = acc
    ab = Bump(A, MIX_COMMON_END)
    watt = [tuple(ab.alloc([8, 128], BF16) for _ in range(3)) for _ in range(2)]
    att_bufs = [dict(qT=ab.alloc([2, S], BF16), kT=ab.alloc([S], BF16), Vaug=ab.alloc([16, 2, 128], BF16))
                for _ in range(2)]
    pT = ab.alloc([3, 512], BF16)
    lnz = ab.alloc([512], F32)
    rz = ab.alloc([512], F32)

    def wview(w2d):
        return w2d.rearrange("(kc p) n -> p kc n", p=128)

    units = []
    state = {"ffn_i": 0, "att_i": 0}

    def plan():
        for s in range(n_seq):
            for l in range(n_layers):
                for gi in range(len(FFN_GROUPS)):
                    units.append(("ffn", s, l, 1, gi))
                units.append(("dt", s, l))
                units.append(("ssd", s, l, 0))
                units.append(("ssd", s, l, 1))
                units.append(("wo", s, l, 0))
                units.append(("ssd", s, l, 2))
                units.append(("ssd", s, l, 3))
                units.append(("wo", s, l, 1))
                for c in range(4):
                    units.append(("att", s, l, c))
                units.append(("wo", s, l, 2))
                for c in range(4, 8):
                    units.append(("att", s, l, c))
                units.append(("wo", s, l, 3))
                for gi in range(len(FFN_GROUPS)):
                    units.append(("ffn", s, l, 2, gi))

    plan()
    loaded = {}
    wpos = {"next_load": 0, "next_get": 0}

    def load_unit(u):
        kind = u[0]
        if kind == "ffn":
            _, s, l, f, gi = u
            J0, J1 = FFN_GROUPS[gi]
            G = J1 - J0
            slot = ffn_slots[state["ffn_i"] % 2]
            state["ffn_i"] += 1
            Wg = slot[0][:, :, 0:G * 128]
            Wu = slot[1][:, :, 0:G * 128]
            Wd = slot[2][:, 0:G, :]
            P.dma(Wg, wview(w_g[f][l])[:, :, J0 * 128:J1 * 128], "pool")
            P.dma(Wu, wview(w_u[f][l])[:, :, J0 * 128:J1 * 128], "pool")
            P.dma(Wd, wview(w_d[f][l])[:, J0:J1, :], "pool")
            return (Wg, Wu, Wd, G)
        if kind == "dt":
            _, s, l = u
            P.dma(wdt, wview(w_in[l])[:, :, DT0:DT0 + 16], "pool")
            return wdt
        if kind == "ssd":
            _, s, l, g = u
            wv = wview(w_in[l])
            P.dma(wssd[0], wv[:, :, X0 + 256 * g: X0 + 256 * g + 256], "pool")
            P.dma(wssd[1], wv[:, :, B0 + 128 * g: B0 + 128 * g + 128], "pool")
            P.dma(wssd[2], wv[:, :, C0 + 128 * g: C0 + 128 * g + 128], "pool")
            P.dma(wssd[3], wv[:, :, Z0 + 256 * g: Z0 + 256 * g + 256], "pool")
            return wssd
        if kind == "att":
            _, s, l, c = u
            wv = wview(w_in[l])
            slot = watt[state["att_i"] % 2]
            state["att_i"] += 1
            P.dma(slot[0], wv[:, :, Q0 + 128 * c: Q0 + 128 * c + 128], "pool")
            P.dma(slot[1], wv[:, :, K0 + 128 * c: K0 + 128 * c + 128], "pool")
            P.dma(slot[2], wv[:, :, V0 + 128 * c: V0 + 128 * c + 128], "pool")
            return slot
        if kind == "wo":
            _, s, l, q4 = u
            P.dma(wo_slot, wview(w_out[l])[:, 4 * q4:4 * q4 + 4, :], "pool")
            return wo_slot
        raise ValueError(kind)

    def wget(u):
        return load_unit(u)

    dumped = set()

    def dump(name, ap):
        if dbg is None or name not in dbg or name in dumped:
            return
        dumped.add(name)
        shp = [int(v) for v in ap.shape]
        d = nc.dram_tensor("dbg_" + name, shp, F32, kind="ExternalOutput").ap()
        P.dma(d, ap, "pool", is_out=True)

    P.dma(cf[:, :], cf_d, "sp")
    P.dma(cb[:, :], cb_d, "pool")
    P.dma(am[:, :, :], am_d.rearrange("p (a b) -> p a b", a=9), "pool")
    P.dma(pv[:, :, :], pvec_d.rearrange("l p n -> p l n"), "sp")

    def pcol(l, c):
        return pv[:, l, c:c + 1]

    def norm_tile(l, base, T):
        tok = slice(T * 512, (T + 1) * 512)
        ss = ps[T % 2]
        for c in range(NCH):
            P.act(sq[:, c % 2, :], x[:, c, tok], AF.Square)
            P.mm(ss[:, :], ones_b, sq[:, c % 2, :], start=(c == 0), stop=(c == NCH - 1))
        P.act(lnv, ss[:, :], AF.Ln, bias=EPS, scale=1.0 / D)
        P.act(rstd[:, T % 2, :], lnv, AF.Exp, scale=-0.5)
        for c in range(NCH):
            P.stt(hT[:, c, tok], x[:, c, tok], pcol(l, base + c), rstd[:, T % 2, :], ALU.mult, ALU.mult)

    def norm_phase(l, base):
        for T in range(4):
            norm_tile(l, base, T)

    def ffn_phase(s, l, f, tail=None):
        for gi in range(len(FFN_GROUPS)):
            Wg, Wu, Wd, G = wget(("ffn", s, l, f, gi))
            last = gi == len(FFN_GROUPS) - 1
            for T in range(4):
                if last and tail is not None and T >= 1:
                    pending_norm = (tail[0], tail[1], T - 1)
                else:
                    pending_norm = None
                tok = slice(T * 512, (T + 1) * 512)
                aT = aTb[T % 2]
                for jj in range(G):
                    gp = ps[jj % 2]
                    up = ps[2 + jj % 2]
                    for kc in range(NCH):
                        P.mm(gp[:, :], Wg[:, kc, jj * 128:(jj + 1) * 128], hT[:, kc, tok], kc == 0, kc == NCH - 1)
                    for kc in range(NCH):
                        P.mm(up[:, :], Wu[:, kc, jj * 128:(jj + 1) * 128], hT[:, kc, tok], kc == 0, kc == NCH - 1)
                    P.act(sgb[jj % 2], gp[:, :], AF.Silu)
                    P.tt(aT[:, jj, :], up[:, :], sgb[jj % 2], ALU.mult)
                for dc in range(NCH):
                    yp = ps[4 + dc % 4]
                    for jj in range(G):
                        P.mm(yp[:, :], Wd[:, jj, dc * 128:(dc + 1) * 128], aT[:, jj, :], jj == 0, jj == G - 1)
                    P.stt(x[:, dc, tok], yp[:, :], 0.5, x[:, dc, tok], ALU.mult, ALU.add)
                if pending_norm is not None:
                    norm_tile(*pending_norm)
            if last and tail is not None:
                norm_tile(tail[0], tail[1], 3)

    def dt_phase(s, l):
        W = wget(("dt", s, l))
        dtp = ps[2]
        for tc in range(16):
            for kc in range(NCH):
                P.mm(dtp[:, tc * 16:(tc + 1) * 16], hT[:, kc, tc * 128:(tc + 1) * 128], W[:, kc, :], kc == 0, kc == NCH - 1)
        v3 = lambda a: a.rearrange("p (a b) -> p a b", a=16)
        bias_b = pv[:, l, PV_DTB:PV_DTB + 16].unsqueeze(1).broadcast_to([128, 16, 16])
        P.tt(v3(dt_tmp), v3(dtp[:, 0:256]), bias_b, ALU.add)
        P.act(dt_tmp, dt_tmp, AF.Exp)
        P.act(dt_tok, dt_tmp, AF.Ln, bias=1.0)
        P.act(a_bc, pv[:, l, PV_ALOG:PV_ALOG + 16], AF.Exp)
        P.stt(v3(adt), v3(dt_tok), -1.0, a_bc.unsqueeze(1).broadcast_to([128, 16, 16]), ALU.mult, ALU.mult)
        P.copy(adt_b, adt, eng="act")
        decp = ps[3]
        cdp = ps[4]
        for tc in range(16):
            P.mm(decp[:, tc * 16:(tc + 1) * 16], su, adt[:, tc * 16:(tc + 1) * 16])
            P.mm(cdp[:, tc * 16:(tc + 1) * 16], ones_f, adt[:, tc * 16:(tc + 1) * 16])
        P.act(decay_tok, decp[:, 0:256], AF.Exp)
        P.act(cd_bc, cdp[:, 0:256], AF.Exp)
        dump("dt_tok", dt_tok); dump("adt", adt); dump("decay_tok", decay_tok); dump("cd_bc", cd_bc)

    tpb = ps[2][:, :].bitcast(BF16)

    ssd_pref = {}

    def ssd_gen(s, l, g):
        W = ssd_pref.pop((s, l, g), None)
        if W is None:
            W = wget(("ssd", s, l, g))
        P.memset(Hst, 0.0)
        P.memset(Hb[:, 0, :], 0.0)
        P.memset(xs[:, :, 0:3], 0.0)
        hcur = 0
        cidx = [2 * g, 2 * g + 1, 8 + g, 12 + g]
        for ch in range(4):
            for k in range(4):
                P.ts(cdiag[:, ch, k, :], ident, pcol(l, PV_CW + k * 16 + cidx[ch]), None, ALU.mult)
        h4 = lambda a, n: a.rearrange("p (h n) -> p h n", h=4)
        def Pm(T, ch):
            tk = slice(T * 512, (T + 1) * 512)
            pp = ps[ch % 2]
            Wsub, col = ((W[0], 0), (W[0], 128), (W[1], 0), (W[2], 0), (W[3], 0), (W[3], 128))[ch]
            for kc in range(NCH):
                P.mm(pp[:, :], Wsub[:, kc, col:col + 128], hT[:, kc, tk], kc == 0, kc == NCH - 1)

        def Cv(T, ch):
            pp = ps[ch % 2]
            P.copy(xs[:, ch, 3:515], pp[:, :], eng="act")
            ci = cidx[ch]
            cps = ps[3 + ch % 2]
            for k in range(4):
                P.mm(cps[:, :], cdiag[:, ch, k, :], xs[:, ch, k:k + 512], k == 0, k == 3)
            P.copy(xs[:, ch, 0:3], xs[:, ch, 512:515])
            if ch < 2:
                P.act(xc[:, T % 2, ch, :], cps[:, :], AF.Silu, bias=pcol(l, PV_CB + ci))
            P.act(xcb[:, ch, :], cps[:, :], AF.Silu, bias=pcol(l, PV_CB + ci))

        def epi(T):
            tk = slice(T * 512, (T + 1) * 512)
            yps_ = (ps[6], ps[7])
            for cc in range(2):
                chunk = 2 * g + cc
                P.stt(t1[:, cc, :], xc[:, T % 2, cc, :], pcol(l, PV_DSKIP + chunk), yps_[cc][:, :], ALU.mult, ALU.add)
                P.tt(t1[:, cc, :], t1[:, cc, :], zs[:, T % 2, cc, :], ALU.mult)
                P.act(sq[:, cc, :], t1[:, cc, :], AF.Square)
                P.mm(ps[3][:, :], ones_b, sq[:, cc, :], cc == 0, cc == 1)
            P.act(lnv, ps[3][:, :], AF.Ln, bias=EPS, scale=1.0 / 256)
            P.act(rstd[:, 0, :], lnv, AF.Exp, scale=-0.5)
            for cc in range(2):
                chunk = 2 * g + cc
                P.stt(yT[:, chunk % 4, tk], t1[:, cc, :], pcol(l, PV_SSDN + chunk), rstd[:, 0, :], ALU.mult, ALU.mult)

        Pm(0, 0)
        Pm(0, 1)
        for T in range(4):
            tok = slice(T * 512, (T + 1) * 512)
            for fn_, a_ in ((Cv, 0), (Pm, 2), (Cv, 1), (Pm, 3), (Cv, 2), (Pm, 4), (Cv, 3), (Pm, 5)):
                fn_(T, a_)
                yield
            if T == 3 and g + 1 < 4:
                ssd_pref[(s, l, g + 1)] = wget(("ssd", s, l, g + 1))
            P.act(zs[:, T % 2, 0, :], ps[0][:, :], AF.Silu)
            P.act(zs[:, T % 2, 1, :], ps[1][:, :], AF.Silu)
            yps = (ps[6], ps[7])
            if T >= 1:
                epi(T - 1)
            hbox = [hcur]

            def stA(tc4):
                c0 = tc4 * 128
                tcg = 4 * T + tc4
                tpo = (tc4 % 2) * 512
                for ch in range(3):
                    P.tr(tpb[:, tpo + ch * 128: tpo + (ch + 1) * 128], xcb[:, ch, c0:c0 + 128], ident)
                hsl = slice(tcg * 16 + 4 * g, tcg * 16 + 4 * g + 4)
                Xt = X_tok[:, tc4 % 3, :]
                Xd = Xd_tok[:, tc4 % 3, :]
                P.tt(h4(Xt, 64), h4(tpb[:, tpo:tpo + 256], 64),
                     dt_tok[:, hsl].unsqueeze(2).broadcast_to([128, 4, 64]), ALU.mult)
                P.copy(B_tok[:, tc4 % 3, :], tpb[:, tpo + 256:tpo + 384], eng="act")
                P.tt(h4(rhsA_b[:, tc4 % 2, :], 128), tri_b.unsqueeze(1).broadcast_to([128, 4, 128]),
                     adt_b[:, hsl].unsqueeze(2).broadcast_to([128, 4, 128]), ALU.mult)
                P.tt(h4(Xd, 64), h4(Xt, 64), decay_tok[:, hsl].unsqueeze(2).broadcast_to([128, 4, 64]), ALU.mult)

            def stB(tc4):
                c0 = tc4 * 128
                rhsA = rhsA_b[:, tc4 % 2, :]
                lmat = lmat_b[:, tc4 % 2, :]
                Ebc = Ebc_b[:, tc4 % 2, :]
                CBm = CBm_b[:, tc4 % 2, :]
                P.mm(ps[3][:, :], su_b, rhsA)
                P.mm(ps[4][:, :], ones_b, rhsA)
                P.mm(ps[5][:, 0:128], xcb[:, 2, c0:c0 + 128], xcb[:, 3, c0:c0 + 128])
                P.act(lmat, ps[3][:, :], AF.Exp)
                P.act(Ebc, ps[4][:, :], AF.Exp)
                P.tt(CBm, ps[5][:, 0:128], tri, ALU.mult)
                P.tt(h4(MT_b[:, tc4 % 2, :], 128), h4(lmat, 128), CBm.unsqueeze(1).broadcast_to([128, 4, 128]), ALU.mult)
                P.tt(h4(Cs_b[:, tc4 % 2, :], 128), h4(Ebc, 128),
                     xcb[:, 3, c0:c0 + 128].unsqueeze(1).broadcast_to([128, 4, 128]), ALU.mult)

            def stC(tc4):
                c0 = tc4 * 128
                tcg = 4 * T + tc4
                hsl = slice(tcg * 16 + 4 * g, tcg * 16 + 4 * g + 4)
                Xt = X_tok[:, tc4 % 3, :]
                Xd = Xd_tok[:, tc4 % 3, :]
                MT = MT_b[:, tc4 % 2, :]
                Cs = Cs_b[:, tc4 % 2, :]
                hc = hbox[0]
                P.mm(ps[5][:, 128:384], B_tok[:, tc4 % 3, :], Xd)
                for h in range(4):
                    cdcol = cd_bc[:, tcg * 16 + 4 * g + h: tcg * 16 + 4 * g + h + 1]
                    P.stt(Hst[:, h * 64:(h + 1) * 64], Hst[:, h * 64:(h + 1) * 64], cdcol,
                          ps[5][:, 128 + h * 64:128 + (h + 1) * 64], ALU.mult, ALU.add)
                P.copy(Hb[:, 1 - hc, :], Hst, eng="act")
                for h in range(4):
                    cc, hh = h // 2, h % 2
                    yo = yps[cc][hh * 64:(hh + 1) * 64, c0:c0 + 128]
                    P.mm(yo, Xt[:, h * 64:(h + 1) * 64], MT[:, h * 128:(h + 1) * 128], True, False)
                    P.mm(yo, Hb[:, hc, h * 64:(h + 1) * 64], Cs[:, h * 128:(h + 1) * 128], False, True)
                hbox[0] = 1 - hc

            nxt = []
            if T < 3:
                tkn = slice((T + 1) * 512, (T + 2) * 512)
                for ch_ in range(2):
                    Wsub_, col_ = ((W[0], 0), (W[0], 128))[ch_]
                    for kc in range(NCH):
                        nxt.append((ps[ch_], Wsub_[:, kc, col_:col_ + 128], hT[:, kc, tkn], kc == 0, kc == NCH - 1))
            for si, (st, i) in enumerate((("A", 0), ("A", 1), ("B", 0), ("A", 2), ("B", 1), ("C", 0), ("A", 3), ("B", 2),
                                          ("C", 1), ("B", 3), ("C", 2), ("C", 3))):
                {"A": stA, "B": stB, "C": stC}[st](i)
                if si >= 4:
                    for _ in range(2):
                        if nxt:
                            o_, l_, r_, st_, sp_ = nxt.pop(0)
                            P.mm(o_[:, :], l_, r_, st_, sp_)
                yield
            for o_, l_, r_, st_, sp_ in nxt:
                P.mm(o_[:, :], l_, r_, st_, sp_)
            hcur = hbox[0]
        epi(3)

    def dump_yT(tag):
        dump("yT_" + tag, yT[:, :, :])

    def att_proj_gen(s, l, c, B):
        qT, kT, Vaug = B["qT"], B["kT"], B["Vaug"]
        W = wget(("att", s, l, c))
        psv = ps[0][:, :].rearrange("p (a h e) -> p a h e", a=4, h=2)
        for T in range(4):
            tok = slice(T * 512, (T + 1) * 512)
            for kc in range(NCH):
                P.mm(ps[0][:, :], W[0][:, kc, :], hT[:, kc, tok], kc == 0, kc == NCH - 1)
            yield
            P.act(sq[:, 0, :], ps[0][:, :], AF.Square)
            for kc in range(NCH):
                P.mm(ps[1][:, :], W[1][:, kc, :], hT[:, kc, tok], kc == 0, kc == NCH - 1)
            yield
            P.act(sq[:, 1, :], ps[1][:, :], AF.Square)
            P.mm(ps[2][:, :], blk_b, sq[:, 0, :])
            yield
            P.act(lnv, ps[2][:, :], AF.Ln, bias=EPS, scale=1.0 / 64)
            P.act(rstd[:, 0, :], lnv, AF.Exp, scale=-0.5)
            yield
            P.mm(ps[2][:, :], blk_b, sq[:, 1, :])
            P.stt(qT[0:64, 0, tok], ps[0][0:64, :], pv[0:64, l, PV_QN:PV_QN + 1], rstd[0:64, 0, :], ALU.mult, ALU.mult)
            P.stt(qT[64:128, 1, tok], ps[0][64:128, :], pv[64:128, l, PV_QN:PV_QN + 1], rstd[64:128, 0, :],
                  ALU.mult, ALU.mult)
            yield
            P.act(lnv, ps[2][:, :], AF.Ln, bias=EPS, scale=1.0 / 64)
            P.act(rstd[:, 1, :], lnv, AF.Exp, scale=-0.5)
            yield
            P.stt(kT[:, tok], ps[1][:, :], pcol(l, PV_KN), rstd[:, 1, :], ALU.mult, ALU.mult)
            for kb4 in range(4):
                kb = 4 * T + kb4
                for kc in range(NCH):
                    P.mm(ps[0][:, kb4 * 128:(kb4 + 1) * 128], hT[:, kc, kb * 128:(kb + 1) * 128], W[2][:, kc, :],
                         kc == 0, kc == NCH - 1)
            yield
            P.copy(Vaug[:, 4 * T:4 * T + 4, 0, 0:64], psv[:, :, 0, :])
            P.copy(Vaug[:, 4 * T:4 * T + 4, 1, 64:128], psv[:, :, 1, :])
            yield

    def att_blocks_gen(s, l, c, B):
        qT, kT, Vaug = B["qT"], B["kT"], B["Vaug"]
        blocks = []
        for hh in range(2):
            for T in range(4):
                nj = 4 * T + 4
                for j in range(nj):
                    blocks.append((hh, T, j, nj))
        sbank = (ps[3], ps[4], ps[5])

        def cols(T, j):
            r = j - 4 * T
            return 128 * r if r > 0 else 0

        def qk(i):
            hh, T, j, nj = blocks[i]
            Up = slice(hh * 64, hh * 64 + 64)
            q0 = cols(T, j)
            P.mm(sbank[i % 3][:, q0:512], kT[:, j * 128:(j + 1) * 128], qT[:, hh, T * 512 + q0:(T + 1) * 512])

        LOOK = 2
        pending = []
        for i in range(min(LOOK, len(blocks))):
            qk(i)
        for i, (hh, T, j, nj) in enumerate(blocks):
            if i + LOOK < len(blocks):
                qk(i + LOOK)
            Up = slice(hh * 64, hh * 64 + 64)
            Zp = slice((1 - hh) * 64, (1 - hh) * 64 + 64)
            tok = slice(T * 512, (T + 1) * 512)
            q0 = cols(T, j)
            op_ = ps[6 + (T % 2)]
            pt = pT[:, i % 3, :]
            P.act(pt[:, q0:512], sbank[i % 3][:, q0:512], AF.Exp, scale=0.125)
            m = 4 * T - j
            mi = m + 3 if m <= 4 else 8
            P.tt(pt[:, q0:512], pt[:, q0:512], am[:, mi, q0:512], ALU.mult)
            P.mm(op_[:, q0:512], Vaug[:, j, hh, :], pt[:, q0:512], j == 0, j == nj - 1)
            if j == nj - 1:
                pending.append((i + 3, Up, Zp, tok, op_))
            while pending and pending[0][0] <= i:
                _, Up_, Zp_, tok_, o_ = pending.pop(0)
                P.act(lnz[Up_, :], o_[Zp_, :], AF.Ln)
                P.act(rz[Up_, :], lnz[Up_, :], AF.Exp, scale=-1.0)
                P.tt(yT[Up_, c % 4, tok_], o_[Up_, :], rz[Up_, :], ALU.mult)
            yield
        for _, Up_, Zp_, tok_, o_ in pending:
            P.act(lnz[Up_, :], o_[Zp_, :], AF.Ln)
            P.act(rz[Up_, :], lnz[Up_, :], AF.Exp, scale=-1.0)
            P.tt(yT[Up_, c % 4, tok_], o_[Up_, :], rz[Up_, :], ALU.mult)

    def interleave(main, sides):
        sides = [[g, r, 0.0, False] for (g, r) in sides if g is not None]
        for _ in main:
            for sd in sides:
                if sd[3]:
                    continue
                sd[2] += sd[1]
                while sd[2] >= 1.0 and not sd[3]:
                    sd[2] -= 1.0
                    try:
                        next(sd[0])
                    except StopIteration:
                        sd[3] = True
        for sd in sides:
            if not sd[3]:
                for _ in sd[0]:
                    pass

    def att_all(s, l, tail=None, with_op1=False):
        for B in att_bufs:
            P.memset(B["Vaug"][:, :, 0, 64:128], 1.0)
            P.memset(B["Vaug"][:, :, 1, 0:64], 1.0)
            P.memset(B["qT"][64:128, 0, :], 0.0)
            P.memset(B["qT"][0:64, 1, :], 0.0)
        if with_op1:
            interleave(att_proj_gen(s, l, 0, att_bufs[0]), [(out_proj_gen(s, l, 1, banks=(3, 4)), 1.0)])
        else:
            run(att_proj_gen(s, l, 0, att_bufs[0]))
        for c in range(8):
            sides = []
            if c + 1 < 8:
                sides.append((att_proj_gen(s, l, c + 1, att_bufs[(c + 1) % 2]), 0.5))
            interleave(att_blocks_gen(s, l, c, att_bufs[c % 2]), sides)
            if c == 3:
                out_proj(s, l, 2)
        out_proj(s, l, 3, tail)

    def out_proj_gen(s, l, q4, tail=None, banks=(0, 1)):
        Wo = wget(("wo", s, l, q4))
        for T in range(4):
            tok = slice(T * 512, (T + 1) * 512)
            for dc in range(NCH):
                yp = ps[banks[dc % 2]]
                for kc in range(4):
                    P.mm(yp[:, :], Wo[:, kc, dc * 128:(dc + 1) * 128], yT[:, kc, tok], kc == 0, kc == 3)
                P.tt(x[:, dc, tok], yp[:, :], x[:, dc, tok], ALU.add)
                yield
            if tail is not None and T >= 1:
                norm_tile(tail[0], tail[1], T - 1)
        if tail is not None:
            norm_tile(tail[0], tail[1], 3)

    def run(gen):
        for _ in gen:
            pass

    def out_proj(s, l, q4, tail=None):
        run(out_proj_gen(s, l, q4, tail))

    nphase = [0]

    def go():
        nphase[0] += 1
        return stop_after is None or nphase[0] <= stop_after

    fuse = (stop_after is None) or FORCE_FUSE

    def layers(s):
        for l in range(n_layers):
            if not go():
                return
            if l == 0 or not fuse:
                norm_phase(l, PV_FFN1)
            ffn_phase(s, l, 1, tail=(l, PV_MIX) if fuse else None)
            if not go():
                return
            if not fuse:
                norm_phase(l, PV_MIX)
            dt_phase(s, l)
            run(ssd_gen(s, l, 0))
            run(ssd_gen(s, l, 1))
            dump_yT("ssd01")
            out_proj(s, l, 0)
            run(ssd_gen(s, l, 2))
            run(ssd_gen(s, l, 3))
            if not fuse:
                out_proj(s, l, 1)
            if not go():
                return
            if fuse:
                att_all(s, l, tail=(l, PV_FFN2), with_op1=True)
            else:
                att_all_plain(s, l)
            if not go():
                return
            if not fuse:
                norm_phase(l, PV_FFN2)
            ffn_phase(s, l, 2, tail=(l + 1, PV_FFN1) if (fuse and l + 1 < n_layers) else None)

    def att_all_plain(s, l):
        for B in att_bufs:
            P.memset(B["Vaug"][:, :, 0, 64:128], 1.0)
            P.memset(B["Vaug"][:, :, 1, 0:64], 1.0)
            P.memset(B["qT"][64:128, 0, :], 0.0)
            P.memset(B["qT"][0:64, 1, :], 0.0)
        run(att_proj_gen(s, l, 0, att_bufs[0]))
        for c in range(8):
            sides = []
            if c + 1 < 8:
                sides.append((att_proj_gen(s, l, c + 1, att_bufs[(c + 1) % 2]), 0.5))
            interleave(att_blocks_gen(s, l, c, att_bufs[c % 2]), sides)
            if c == 3:
                out_proj(s, l, 2)
        out_proj(s, l, 3)

    for s in range(n_seq):
        for c in range(NCH):
            P.dma(x[:, c, :], xT[s, c * 128:(c + 1) * 128, :], "sp")
        layers(s)
        for c in range(NCH):
            P.dma(outT[s, c * 128:(c + 1) * 128, :], x[:, c, :], "sp", is_out=True)
        if stop_after is not None:
            break
    P.emit(es)
    es.close()
    return nc


def make_consts():
    k = np.arange(128)
    tri = (k[:, None] <= k[None, :]).astype(np.float32)
    su = (k[:, None] > k[None, :]).astype(np.float32)
    ones = np.ones((128, 128), np.float32)
    cf = np.concatenate([tri, su, ones], axis=1)
    ident = np.eye(128, dtype=np.float32)
    blk = np.zeros((128, 128), np.float32)
    blk[:64, :64] = 1.0
    blk[64:, 64:] = 1.0
    cb = np.concatenate([ident, ones, blk, su, tri], axis=1)
    am = np.zeros((128, 9, 512), np.float32)
    kk = np.arange(128)[:, None]
    qq = np.arange(512)[None, :]
    for mi in range(8):
        m = mi - 3
        d = 128 * m + qq - kk
        c = ((d >= 0) & (d <= 128)).astype(np.float32)
        c += ((d >= 0) & (d % 4 == 0) & (d <= 512)).astype(np.float32)
        c += ((d >= 0) & (d % 16 == 0)).astype(np.float32)
        am[:, mi, :] = c
    am[:, 8, :] = (((qq - kk) % 16) == 0).astype(np.float32)
    return cf, cb, np.ascontiguousarray(am.reshape(128, 9 * 512))


def make_pvec(inp):
    pv = np.zeros((2, 128, NV), np.float32)
    p = np.arange(128)
    for l in range(2):
        def cols(v, n):
            return np.asarray(v, np.float32).reshape(n, 128).T
        pv[l, :, PV_FFN1:PV_FFN1 + 8] = cols(inp["ffn1_norm"][l], 8)
        pv[l, :, PV_MIX:PV_MIX + 8] = cols(inp["mix_norm"][l], 8)
        pv[l, :, PV_FFN2:PV_FFN2 + 8] = cols(inp["ffn2_norm"][l], 8)
        pv[l, :, PV_SSDN:PV_SSDN + 8] = cols(inp["ssd_norm"][l], 8)
        ds = np.asarray(inp["d_skip"][l], np.float32)
        for c in range(8):
            pv[l, :, PV_DSKIP + c] = ds[2 * c + p // 64]
        pv[l, :, PV_QN] = np.asarray(inp["q_norm"][l], np.float32)[p % 64]
        pv[l, :, PV_KN] = np.asarray(inp["k_norm"][l], np.float32)[p % 64]
        pv[l, :, PV_CB:PV_CB + 16] = cols(inp["conv_b"][l], 16)
        cw = np.asarray(inp["conv_w"][l], np.float32)
        for k in range(4):
            pv[l, :, PV_CW + k * 16:PV_CW + (k + 1) * 16] = cols(cw[k], 16)
        pv[l, :, PV_DTB:PV_DTB + 16] = np.asarray(inp["dt_bias"][l], np.float32)[None, :]
        pv[l, :, PV_ALOG:PV_ALOG + 16] = np.asarray(inp["a_log"][l], np.float32)[None, :]
    return pv


_W_NAMES = ["ffn1_w_gate", "ffn1_w_up", "ffn1_w_down", "ffn2_w_gate", "ffn2_w_up", "ffn2_w_down", "w_in", "w_out"]


def kernel(**inputs):
    x = np.asarray(inputs["x"], np.float32)
    nb = x.shape[0]
    assert nb == N_CORES * SEQ_PER_CORE
    cf, cb, am = make_consts()
    pvec = make_pvec(inputs)
    shared = {n: np.ascontiguousarray(np.asarray(inputs[n], np.float32)) for n in _W_NAMES}
    shared.update({"pvec": pvec, "cf": cf, "cb": cb, "am": am})
    in_maps = []
    for core in range(N_CORES):
        xs = x[core * SEQ_PER_CORE:(core + 1) * SEQ_PER_CORE]
        m = dict(shared)
        m["xT"] = np.ascontiguousarray(xs.transpose(0, 2, 1))
        in_maps.append(m)
    nc = build_program()
    res = run_bass_kernel_spmd(nc, in_maps, core_ids=list(range(N_CORES)))
    out = np.empty((nb, S, D), np.float32)
    for core in range(N_CORES):
        o = np.asarray(res.results[core]["outT"])
        out[core * SEQ_PER_CORE:(core + 1) * SEQ_PER_CORE] = o.transpose(0, 2, 1)
    return out
```
